# Optimizing a Trainium2 kernel written in Bass

```python
import numpy as np
import jax, jax.numpy as jnp
from jax import lax

D_MODEL = 1024
BATCH = 16
SEQ = 4096
DEPTH = 4

CTX_LEN = 256
GRID_W = 64
D_MIX = D_MODEL
HEAD_DIM = 64
ATTN_HEADS = D_MIX // 2 // HEAD_DIM
ATTN_KV_HEADS = ATTN_HEADS // 4
ATTN_GROUP = ATTN_HEADS // ATTN_KV_HEADS
WINDOW = 128
ATTN_BLOCK = 128
ROPE_BASE = 10000.0
ROPE_AXIS_DIM = HEAD_DIM // 2
M_HEADS = D_MIX // 4 // HEAD_DIM
M_WIDTH = M_HEADS * HEAD_DIM
M_CHUNK = 64
G_HEADS = D_MIX // 4 // HEAD_DIM
G_WIDTH = G_HEADS * HEAD_DIM
G_CHUNK = 64
GLA_RANK = 16
GLA_TAU = 16.0
D_FF = 4 * D_MODEL
EPS = 1e-6
IN_SPLITS = (ATTN_HEADS * HEAD_DIM, ATTN_KV_HEADS * HEAD_DIM, ATTN_KV_HEADS * HEAD_DIM,
             M_WIDTH, M_WIDTH, M_WIDTH, M_WIDTH, 2 * M_HEADS, 2 * M_HEADS,
             G_WIDTH, G_WIDTH, G_WIDTH, G_WIDTH, 2 * GLA_RANK)
IN_COLS = sum(IN_SPLITS)

kernel_name = 'hymba_style_flow_backbone'

F32 = jnp.float32


def rmsnorm(x, g):
    xf = x.astype(F32)
    xf = xf * lax.rsqrt(jnp.mean(xf * xf, axis=-1, keepdims=True) + EPS)
    return (xf * g.astype(F32)).astype(x.dtype)


def head_rmsnorm(x, g, n_heads):
    B, N, _ = x.shape
    xh = x.reshape(B, N, n_heads, HEAD_DIM).astype(F32)
    xh = xh * lax.rsqrt(jnp.mean(xh * xh, axis=-1, keepdims=True) + EPS)
    return (xh.reshape(B, N, n_heads * HEAD_DIM) * g.astype(F32)).astype(x.dtype)


def modulate(h, shift, scale):
    return h * (1.0 + scale) + shift


def squared_relu_mlp(h, w1, w2):
    return jnp.square(jax.nn.relu(h @ w1)) @ w2


def split_cols(p):
    idx = np.cumsum(IN_SPLITS)[:-1].tolist()
    return jnp.split(p, idx, axis=-1)


def to_heads(t, n_heads):
    B, N, _ = t.shape
    return t.reshape(B, N, n_heads, HEAD_DIM).transpose(0, 2, 1, 3)


def from_heads(t):
    B, H, N, D = t.shape
    return t.transpose(0, 2, 1, 3).reshape(B, N, H * D)


def axial_rope_tables(rows):
    t_row = jnp.repeat(jnp.arange(rows), GRID_W).astype(F32)
    t_col = jnp.tile(jnp.arange(GRID_W), rows).astype(F32)
    inv = ROPE_BASE ** (-jnp.arange(0, ROPE_AXIS_DIM, 2, dtype=F32) / ROPE_AXIS_DIM)
    ang_r = t_row[:, None] * inv[None, :]
    ang_c = t_col[:, None] * inv[None, :]
    return (jnp.cos(ang_r), jnp.sin(ang_r), jnp.cos(ang_c), jnp.sin(ang_c))


def rotate_axis(xa, cos, sin):
    half = xa.shape[-1] // 2
    x1, x2 = xa[..., :half], xa[..., half:]
    cos = cos[None, :, None, :].astype(xa.dtype)
    sin = sin[None, :, None, :].astype(xa.dtype)
    return jnp.concatenate([x1 * cos - x2 * sin, x2 * cos + x1 * sin], axis=-1)


def apply_axial_rope(x, rope):
    cos_r, sin_r, cos_c, sin_c = rope
    return jnp.concatenate([rotate_axis(x[..., :ROPE_AXIS_DIM], cos_r, sin_r),
                            rotate_axis(x[..., ROPE_AXIS_DIM:], cos_c, sin_c)], axis=-1)


def sink_softmax(s, sink):
    sk = sink.astype(F32)[None, :, :, None, None]
    m = jnp.maximum(jnp.max(s, axis=-1, keepdims=True), sk)
    e = jnp.exp(s - m)
    return e / (jnp.sum(e, axis=-1, keepdims=True) + jnp.exp(sk - m))


def context_attention(q, k, v, sink):
    B, Nc = q.shape[:2]
    s = jnp.einsum('bqhgd,bkhd->bhgqk', q, k).astype(F32)
    p = sink_softmax(s, sink).astype(v.dtype)
    return jnp.einsum('bhgqk,bkhd->bqhgd', p, v).reshape(B, Nc, -1)


def window_attention(q, k, v, kc, vc, sink):
    B, N, KV, G, D = q.shape
    nb = N // ATTN_BLOCK
    pad = ((0, 0), (ATTN_BLOCK, ATTN_BLOCK), (0, 0), (0, 0))
    kp = jnp.pad(k, pad)
    vp = jnp.pad(v, pad)
    qb = q.reshape(B, nb, ATTN_BLOCK, KV, G, D).transpose(1, 0, 2, 3, 4, 5)
    q_offs = jnp.arange(ATTN_BLOCK)
    k_offs = jnp.arange(3 * ATTN_BLOCK)

    def block(args):
        j, qj = args
        kj = lax.dynamic_slice_in_dim(kp, j * ATTN_BLOCK, 3 * ATTN_BLOCK, axis=1)
        vj = lax.dynamic_slice_in_dim(vp, j * ATTN_BLOCK, 3 * ATTN_BLOCK, axis=1)
        qpos = j * ATTN_BLOCK + q_offs
        kpos = (j - 1) * ATTN_BLOCK + k_offs
        valid = ((jnp.abs(qpos[:, None] - kpos[None, :]) <= WINDOW)
                 & (kpos >= 0)[None, :] & (kpos < N)[None, :])
        s_loc = jnp.einsum('bqhgd,bkhd->bhgqk', qj, kj).astype(F32)
        s_loc = jnp.where(valid, s_loc, -jnp.inf)
        s_ctx = jnp.einsum('bqhgd,bkhd->bhgqk', qj, kc).astype(F32)
        p = sink_softmax(jnp.concatenate([s_loc, s_ctx], axis=-1), sink).astype(v.dtype)
        return (jnp.einsum('bhgqk,bkhd->bqhgd', p[..., :3 * ATTN_BLOCK], vj)
                + jnp.einsum('bhgqk,bkhd->bqhgd', p[..., 3 * ATTN_BLOCK:], vc))

    o = lax.map(block, (jnp.arange(nb), qb))
    return o.transpose(1, 0, 2, 3, 4, 5).reshape(B, N, KV * G * D)


def mlstm_chunked(q, k, v, logi, logf, state, with_h):
    dt = v.dtype
    B, H, N, D = q.shape
    nc = N // M_CHUNK
    q = q.astype(F32).reshape(B, H, nc, M_CHUNK, D)
    k = (k.astype(F32) * D ** -0.5).reshape(B, H, nc, M_CHUNK, D)
    v = v.astype(F32).reshape(B, H, nc, M_CHUNK, D)
    logi = logi.astype(F32).reshape(B, H, nc, M_CHUNK)
    b = jnp.cumsum(logf.astype(F32).reshape(B, H, nc, M_CHUNK), axis=-1)
    a = b[..., -1]
    g = a[..., None] - b + logi
    m_chunk = jnp.max(g, axis=-1)
    w = jnp.exp(g - m_chunk[..., None])
    c_chunk = jnp.einsum('bhcs,bhcsd,bhcse->bhcde', w, v, k)
    n_chunk = jnp.einsum('bhcs,bhcse->bhce', w, k)
    if state is None:
        state = (jnp.zeros((B, H, D, D), F32), jnp.zeros((B, H, D), F32), jnp.zeros((B, H), F32))

    def step(carry, inp):
        C, n, m = carry
        a_i, m_i, c_i, n_i = inp
        m_new = jnp.maximum(a_i + m, m_i)
        s_old = jnp.exp(a_i + m - m_new)
        s_new = jnp.exp(m_i - m_new)
        C_new = s_old[..., None, None] * C + s_new[..., None, None] * c_i
        n_new = s_old[..., None] * n + s_new[..., None] * n_i
        return (C_new, n_new, m_new), (C, n, m)

    xs = tuple(jnp.moveaxis(t, 2, 0) for t in (a, m_chunk, c_chunk, n_chunk))
    final, prev = lax.scan(step, state, xs)
    if not with_h:
        return None, final
    c_prev, n_prev, m_prev = (jnp.moveaxis(t, 0, 2) for t in prev)
    lower = jnp.tril(jnp.ones((M_CHUNK, M_CHUNK), bool))
    dmat = jnp.where(lower, b[..., :, None] - b[..., None, :] + logi[..., None, :], -jnp.inf)
    inter = b + m_prev[..., None]
    m_r = jnp.maximum(inter, jnp.max(dmat, axis=-1))
    p = jnp.exp(dmat - m_r[..., None]) * jnp.einsum('bhcrd,bhcsd->bhcrs', q, k)
    w_inter = jnp.exp(inter - m_r)
    num = (w_inter[..., None] * jnp.einsum('bhcde,bhcre->bhcrd', c_prev, q)
           + jnp.einsum('bhcrs,bhcsd->bhcrd', p, v))
    den = w_inter * jnp.einsum('bhce,bhcre->bhcr', n_prev, q) + jnp.sum(p, axis=-1)
    h = num / jnp.maximum(jnp.abs(den), jnp.exp(-m_r))[..., None]
    return h.reshape(B, H, N, D).astype(dt), final


def gla_chunked(q, k, v, glog, state, with_h):
    dt = v.dtype
    B, H, N, D = q.shape
    nc = N // G_CHUNK
    q = (q.astype(F32) * D ** -0.5).reshape(B, H, nc, G_CHUNK, D)
    k = k.astype(F32).reshape(B, H, nc, G_CHUNK, D)
    v = v.astype(F32).reshape(B, H, nc, G_CHUNK, D)
    bc = jnp.cumsum(glog.astype(F32).reshape(B, H, nc, G_CHUNK, D), axis=3)
    b_end = bc[:, :, :, -1:, :]
    u = jnp.einsum('bhcsk,bhcsv->bhckv', k * jnp.exp(b_end - bc), v)
    decay = jnp.exp(b_end[:, :, :, 0, :])
    if state is None:
        state = jnp.zeros((B, H, D, D), F32)

    def step(S, inp):
        d_i, u_i = inp
        return d_i[..., None] * S + u_i, S

    final, s_prev = lax.scan(step, state, (jnp.moveaxis(decay, 2, 0), jnp.moveaxis(u, 2, 0)))
    if not with_h:
        return None, final
    s_prev = jnp.moveaxis(s_prev, 0, 2)
    qd = q * jnp.exp(bc)
    kd = k * jnp.exp(-bc)
    lower = jnp.tril(jnp.ones((G_CHUNK, G_CHUNK), bool))
    att = jnp.where(lower, jnp.einsum('bhcrk,bhcsk->bhcrs', qd, kd), 0.0)
    o = jnp.einsum('bhcrs,bhcsv->bhcrv', att, v) + jnp.einsum('bhcrk,bhckv->bhcrv', qd, s_prev)
    return o.reshape(B, H, N, D).astype(dt), final


def run_direction(scan_fn, ctx_args, lat_args, flip, with_ctx_out):
    if flip:
        ctx_args = tuple(jnp.flip(t, axis=2) for t in ctx_args)
        lat_args = tuple(jnp.flip(t, axis=2) for t in lat_args)
    h_ctx, ctx_state = scan_fn(*ctx_args, None, with_ctx_out)
    h_lat, _ = scan_fn(*lat_args, ctx_state, True)
    if flip:
        h_lat = jnp.flip(h_lat, axis=2)
        if h_ctx is not None:
            h_ctx = jnp.flip(h_ctx, axis=2)
    return h_ctx, h_lat


def mixer_inputs(p, m_i_bias, m_f_bias, g_wa2, g_ba):
    aq, ak, av, mq, mk, mv, mo, mi, mf, gq, gk, gv, gg, glr = split_cols(p)
    B, N, _ = p.shape
    attn = (aq.reshape(B, N, ATTN_HEADS, HEAD_DIM),
            ak.reshape(B, N, ATTN_KV_HEADS, HEAD_DIM),
            av.reshape(B, N, ATTN_KV_HEADS, HEAD_DIM))
    logi = (mi.reshape(B, N, 2, M_HEADS).astype(F32) + m_i_bias.astype(F32)).transpose(2, 0, 3, 1)
    logf = jax.nn.log_sigmoid(mf.reshape(B, N, 2, M_HEADS).astype(F32)
                              + m_f_bias.astype(F32)).transpose(2, 0, 3, 1)
    mlstm = (to_heads(mq, M_HEADS), to_heads(mk, M_HEADS), to_heads(mv, M_HEADS), logi, logf)
    glog = jax.nn.log_sigmoid(jnp.einsum('bnjr,jrw->jbnw', glr.reshape(B, N, 2, GLA_RANK).astype(F32),
                                         g_wa2.astype(F32))
                              + g_ba.astype(F32)[:, None, None, :]) / GLA_TAU
    glog = glog.reshape(2, B, N, G_HEADS, HEAD_DIM).transpose(0, 1, 3, 2, 4)
    gla = (to_heads(gq, G_HEADS), to_heads(gk, G_HEADS), to_heads(gv, G_HEADS), glog)
    return attn, mlstm, jax.nn.sigmoid(mo), gla, jax.nn.silu(gg)


def token_mixer(h, hc, rope, w_in, sink, m_i_bias, m_f_bias, m_norm_g, g_wa2, g_ba, g_norm_g, w_out,
                with_ctx_out):
    B, N, _ = h.shape
    Nc = hc.shape[1]
    (aq, ak, av), (mq, mk, mv, mi, mf), m_o, (gq, gk, gv, gl), g_gate = mixer_inputs(
        h @ w_in, m_i_bias, m_f_bias, g_wa2, g_ba)
    (aqc, akc, avc), (mqc, mkc, mvc, mic, mfc), m_oc, (gqc, gkc, gvc, glc), g_gatec = mixer_inputs(
        hc @ w_in, m_i_bias, m_f_bias, g_wa2, g_ba)
    sink_g = sink.reshape(ATTN_KV_HEADS, ATTN_GROUP)
    q = (apply_axial_rope(aq, rope) * HEAD_DIM ** -0.5).reshape(B, N, ATTN_KV_HEADS, ATTN_GROUP, HEAD_DIM)
    k = apply_axial_rope(ak, rope)
    out_a = window_attention(q, k, av, akc, avc, sink_g)
    hm = [run_direction(mlstm_chunked, (mqc, mkc, mvc, mic[d], mfc[d]), (mq, mk, mv, mi[d], mf[d]),
                        flip, with_ctx_out) for d, flip in ((0, False), (1, True))]
    out_m = head_rmsnorm(from_heads(hm[0][1] + hm[1][1]), m_norm_g, M_HEADS) * m_o
    hg = [run_direction(gla_chunked, (gqc, gkc, gvc, glc[d]), (gq, gk, gv, gl[d]),
                        flip, with_ctx_out) for d, flip in ((0, False), (1, True))]
    out_g = head_rmsnorm(from_heads(hg[0][1] + hg[1][1]), g_norm_g, G_HEADS) * g_gate
    o = jnp.concatenate([out_a, out_m, out_g], axis=-1) @ w_out
    if not with_ctx_out:
        return o, None
    qc = (aqc * HEAD_DIM ** -0.5).reshape(B, Nc, ATTN_KV_HEADS, ATTN_GROUP, HEAD_DIM)
    out_ac = context_attention(qc, akc, avc, sink_g)
    out_mc = head_rmsnorm(from_heads(hm[0][0] + hm[1][0]), m_norm_g, M_HEADS) * m_oc
    out_gc = head_rmsnorm(from_heads(hg[0][0] + hg[1][0]), g_norm_g, G_HEADS) * g_gatec
    oc = jnp.concatenate([out_ac, out_mc, out_gc], axis=-1) @ w_out
    return o, oc


def setup_inputs(seed: int = 0) -> dict:
    key = jax.random.key(seed)
    ks = jax.random.split(key, 20)

    def nrm(k, shape, scale):
        return jax.random.normal(k, shape, F32) * scale

    return {
        'x': nrm(ks[0], (BATCH, SEQ, D_MODEL), 1.0),
        'c': nrm(ks[1], (BATCH, D_MODEL), 1.0),
        'ctx': nrm(ks[2], (BATCH, CTX_LEN, D_MODEL), 1.0),
        'c_ctx': nrm(ks[3], (D_MODEL,), 1.0),
        'w_ada': nrm(ks[4], (DEPTH, D_MODEL, 6 * D_MODEL), 0.5 * D_MODEL ** -0.5),
        'b_ada': nrm(ks[5], (DEPTH, 6 * D_MODEL), 0.02),
        'norm1_g': 1.0 + nrm(ks[6], (DEPTH, D_MODEL), 0.05),
        'norm2_g': 1.0 + nrm(ks[7], (DEPTH, D_MODEL), 0.05),
        'w_in': nrm(ks[8], (DEPTH, D_MODEL, IN_COLS), D_MODEL ** -0.5),
        'attn_sink': nrm(ks[9], (DEPTH, ATTN_HEADS), 0.5),
        'm_i_bias': nrm(ks[10], (DEPTH, 2, M_HEADS), 0.1),
        'm_f_bias': 3.0 + 3.0 * jax.random.uniform(ks[11], (DEPTH, 2, M_HEADS), F32),
        'm_norm_g': 1.0 + nrm(ks[12], (DEPTH, M_WIDTH), 0.05),
        'g_wa2': nrm(ks[13], (DEPTH, 2, GLA_RANK, G_WIDTH), GLA_RANK ** -0.5),
        'g_ba': nrm(ks[14], (DEPTH, 2, G_WIDTH), 0.1),
        'g_norm_g': 1.0 + nrm(ks[15], (DEPTH, G_WIDTH), 0.05),
        'w_out': nrm(ks[16], (DEPTH, D_MIX, D_MODEL), D_MIX ** -0.5),
        'w_mlp1': nrm(ks[17], (DEPTH, D_MODEL, D_FF), D_MODEL ** -0.5),
        'w_mlp2': nrm(ks[18], (DEPTH, D_FF, D_MODEL), D_FF ** -0.5),
        'final_g': 1.0 + nrm(ks[19], (D_MODEL,), 0.05),
    }


def reference(x, c, ctx, c_ctx, w_ada, b_ada, norm1_g, norm2_g, w_in, attn_sink, m_i_bias, m_f_bias,
              m_norm_g, g_wa2, g_ba, g_norm_g, w_out, w_mlp1, w_mlp2, final_g):
    B, N, D = x.shape
    ROWS = N // GRID_W
    rope = axial_rope_tables(ROWS)
    sc = jax.nn.silu(c)
    sctx = jax.nn.silu(c_ctx)
    xc = ctx
    for l in range(DEPTH):
        last = l == DEPTH - 1
        mx = (sc @ w_ada[l] + b_ada[l]).reshape(B, 6, 1, D)
        mc = (sctx @ w_ada[l] + b_ada[l]).reshape(6, D)
        h = modulate(rmsnorm(x, norm1_g[l]), mx[:, 0], mx[:, 1])
        hc = modulate(rmsnorm(xc, norm1_g[l]), mc[0], mc[1])
        o, oc = token_mixer(h, hc, rope, w_in[l], attn_sink[l], m_i_bias[l], m_f_bias[l], m_norm_g[l],
                            g_wa2[l], g_ba[l], g_norm_g[l], w_out[l], not last)
        x = x + mx[:, 2] * o
        x = x + mx[:, 5] * squared_relu_mlp(modulate(rmsnorm(x, norm2_g[l]), mx[:, 3], mx[:, 4]),
                                            w_mlp1[l], w_mlp2[l])
        if not last:
            xc = xc + mc[2] * oc
            xc = xc + mc[5] * squared_relu_mlp(modulate(rmsnorm(xc, norm2_g[l]), mc[3], mc[4]),
                                               w_mlp1[l], w_mlp2[l])
    return rmsnorm(x, final_g)
```

```python
import numpy as np
from contextlib import ExitStack
import concourse.bass as bass
import concourse.mybir as mybir
from concourse.bass_utils import run_bass_kernel_spmd

F32 = mybir.dt.float32
BF16 = mybir.dt.bfloat16
AF = mybir.ActivationFunctionType
ALU = mybir.AluOpType
AX = mybir.AxisListType

D = 1024
KC = 8
HD = 64
DFF = 4096
HC = 32
INC = 2864
EPS = 1e-6
LN8 = float(np.log(0.125))

WIN_MAP = [(0, 512, 0), (512, 768, 512), (1792, 1808, 768), (2832, 2864, 784),
           (768, 1280, 816), (1280, 1792, 1328), (1808, 2320, 1840), (2320, 2832, 2352)]
GA, GB, GC, GD, GE, GF = (0, 512), (512, 816), (816, 1328), (1328, 1840), (1840, 2352), (2352, 2864)


class Tok:
    __slots__ = ("name", "w", "r", "excl")

    def __init__(self, name):
        self.name = name
        self.w = None
        self.r = {}
        self.excl = False


class TT:
    def __init__(self, t, name):
        self.t = t
        self.tok = Tok(name)
        self.name = name

    def __getitem__(self, idx):
        return self.t[idx]


class Eng:
    def __init__(self, name, h, sem, self_sync=True):
        self.name = name
        self.h = h
        self.sem = sem
        self.cnt = 0
        self.known = {}
        self.self_sync = self_sync


def _tk(x):
    return x.tok if hasattr(x, "tok") else x


class MK:
    def __init__(self, NB, NTL, NTC, DEPTH):
        self.NB, self.NTL, self.NTC, self.DEPTH = NB, NTL, NTC, DEPTH
        self.NT = NTL + NTC
        self.T = self.NT * 128
        self.N = NTL * 128
        self.NG = NB + 1
        self.nc = bass.Bass('TRN2', target_bir_lowering=False)
        self.stack = ExitStack()
        nc = self.nc
        S = self.stack.enter_context
        self.pe = Eng('pe', nc.tensor, S(nc.semaphore("s_pe")), self_sync=False)
        self.act = Eng('act', nc.scalar, S(nc.semaphore("s_act")))
        self.dve = Eng('dve', nc.vector, S(nc.semaphore("s_dve")))
        self.pool = Eng('pool', nc.gpsimd, S(nc.semaphore("s_pool")))
        self.sp = Eng('sp', nc.sync, None)
        self.engs = [self.pe, self.act, self.dve, self.pool]
        S(nc.allow_low_precision("bf16 matmul operands, fp32 accumulation"))
        self.QK = 8
        self.qsems = [S(nc.semaphore(f"s_q{i}")) for i in range(self.QK)]
        self.qcnt = [0] * self.QK
        self.qidx = 0
        self.dtok = {}
        self.rr = 0

    def _waits(self, eng, reads, writes):
        need = {}

        def add(ev):
            if ev is None:
                return
            sem, val, src = ev
            if src is eng and not eng.self_sync:
                return
            k = id(sem)
            if k not in need or need[k][1] < val:
                need[k] = (sem, val)

        for t in reads:
            t = _tk(t)
            add(t.w)
            if t.excl:
                for ev in t.r.values():
                    if ev[2] is not eng:
                        add(ev)
        for t in writes:
            t = _tk(t)
            add(t.w)
            for ev in t.r.values():
                add(ev)
        for k, (sem, val) in need.items():
            if eng.known.get(k, 0) < val:
                eng.h.wait_ge(sem, val)
                eng.known[k] = val

    def _record(self, ev, reads, writes):
        for t in reads:
            _tk(t).r[id(ev[0])] = ev
        for t in writes:
            t = _tk(t)
            t.w = ev
            t.r = {}

    def ck(self, tag):
        import os
        if os.environ.get("MK_STOP", "") == tag:
            self.stopped = True

    def op(self, eng, fn, R=(), W=()):
        if getattr(self, "stopped", False):
            return
        self._waits(eng, R, W)
        inst = fn()
        eng.cnt += 1
        inst.then_inc(eng.sem, 1)
        self._record((eng.sem, eng.cnt, eng), R, W)

    def dma(self, out, in_, R=(), W=()):
        if getattr(self, "stopped", False):
            return
        eng = self.sp
        self._waits(eng, R, W)
        slot = self.qidx % self.QK
        self.qidx += 1
        sem = self.qsems[slot]
        if self.qcnt[slot] > 0 and eng.known.get(id(sem), 0) < self.qcnt[slot]:
            eng.h.wait_ge(sem, self.qcnt[slot])
            eng.known[id(sem)] = self.qcnt[slot]
        inst = eng.h.dma_start(out=out, in_=in_)
        self.qcnt[slot] += 16
        inst.then_inc(sem, 16)
        self._record((sem, self.qcnt[slot], None), R, W)

    def barrier(self):
        evs = [(e.sem, e.cnt) for e in self.engs if e.cnt > 0]
        evs += [(self.qsems[i], self.qcnt[i]) for i in range(self.QK) if self.qcnt[i] > 0]
        for e in self.engs + [self.sp]:
            for sem, val in evs:
                if e.known.get(id(sem), 0) < val:
                    e.h.wait_ge(sem, val)
                    e.known[id(sem)] = val

    def dt(self, *key):
        if key not in self.dtok:
            self.dtok[key] = Tok(str(key))
        return self.dtok[key]

    def sb(self, name, shape, dt=F32, stack=None):
        self.uid = getattr(self, "uid", 0) + 1
        name = f"{name}_{self.uid}"
        t = (stack or self.stack).enter_context(self.nc.sbuf_tensor(name, list(shape), dt))
        return TT(t, name)

    def psb(self, name, stack=None):
        t = (stack or self.stack).enter_context(self.nc.psum_tensor(name, [128, 512], F32))
        r = TT(t, name)
        r.tok.excl = True
        return r

    def mm(self, out, lhsT, rhs, start, stop, R, W):
        self.op(self.pe, lambda: self.nc.tensor.matmul(out, lhsT=lhsT, rhs=rhs, start=start, stop=stop), R, W)

    def tr(self, out, in_, ident, R, W):
        self.op(self.pe, lambda: self.nc.tensor.transpose(out=out, in_=in_, identity=ident), R, W)

    def actf(self, out, in_, func, R, W, scale=None, bias=None):
        kw = {}
        if scale is not None:
            kw["scale"] = scale
        if bias is not None:
            kw["bias"] = bias
        if bias is not None and not isinstance(bias, float):
            R = list(R) + [self.cst]
        self.op(self.act, lambda: self.nc.scalar.activation(out=out, in_=in_, func=func, **kw), R, W)

    def tt(self, eng, out, in0, in1, op, R, W):
        self.op(eng, lambda: eng.h.tensor_tensor(out=out, in0=in0, in1=in1, op=op), R, W)

    def ts(self, eng, out, in0, s1, op0, R, W, s2=None, op1=None):
        if op1 is None:
            self.op(eng, lambda: eng.h.tensor_scalar(out=out, in0=in0, scalar1=s1, scalar2=None, op0=op0), R, W)
        else:
            self.op(eng, lambda: eng.h.tensor_scalar(out=out, in0=in0, scalar1=s1, scalar2=s2, op0=op0, op1=op1), R, W)

    def stt(self, out, in0, scalar, in1, op0, op1, R, W):
        self.op(self.dve, lambda: self.nc.vector.scalar_tensor_tensor(out=out, in0=in0, scalar=scalar, in1=in1,
                                                                     op0=op0, op1=op1), R, W)

    def cp(self, eng, out, in_, R, W):
        if eng is self.act:
            self.op(eng, lambda: self.nc.scalar.copy(out=out, in_=in_), R, W)
        else:
            self.op(eng, lambda: eng.h.tensor_copy(out=out, in_=in_), R, W)

    def recip(self, out, in_, R, W):
        self.op(self.dve, lambda: self.nc.vector.reciprocal(out=out, in_=in_), R, W)

    def mset(self, eng, ap, val, W):
        self.op(eng, lambda: eng.h.memset(ap, val), (), W)

    def rot(self, engs):
        self.rr += 1
        return engs[self.rr % len(engs)]

    def build(self):
        nc = self.nc
        NB, NT, NTL, NTC, T, N, NG, L = self.NB, self.NT, self.NTL, self.NTC, self.T, self.N, self.NG, self.DEPTH
        dr = lambda name, shape, dt=F32, kind="ExternalInput": nc.dram_tensor(name, list(shape), dt, kind=kind).ap()
        self.xT0 = dr("xT0", [NB, KC, 128, T])
        self.cT = dr("cT", [128, KC, NG])
        self.w_ada = dr("w_ada", [L, D, 6 * D])
        self.b_adaT = dr("b_adaT", [L, 128, 48])
        self.n1g = dr("n1g", [L, 128, KC])
        self.n2g = dr("n2g", [L, 128, KC])
        self.fing = dr("fing", [128, KC])
        self.w_in = dr("w_in", [L, D, INC])
        self.w_out = dr("w_out", [L, D, D])
        self.w_mlp1 = dr("w_mlp1", [L, D, DFF])
        self.w_mlp2 = dr("w_mlp2", [L, DFF, D])
        self.sink = dr("sink", [L, 8])
        self.mib = dr("mib", [L, 8])
        self.mfb = dr("mfb", [L, 8])
        self.mgn = dr("mgn", [L, 512])
        self.waug = dr("waug", [L, 33, 512])
        self.cf32 = dr("cf32", [128, 7 * 128])
        self.rope = dr("rope", [NTL, 128, 4, 64])
        self.outT = dr("outT", [NB, KC, 128, N], kind="ExternalOutput")
        I = "Internal"
        self.xres = dr("xres", [NB, KC, 128, T], kind=I)
        self.s_hf = dr("s_hf", [NB, NT, 128, 512], kind=I)
        self.s_oin = dr("s_oin", [NB, NT, 128, 520], kind=I)
        self.s_u = dr("s_u", [NB, NT, 64, 520], kind=I)
        self.s_dec = dr("s_dec", [NB, NT, 64, 8], kind=I)
        self.s_qT = dr("s_qT", [NB, NT, 64, 1024], BF16, kind=I)
        self.s_gate = dr("s_gate", [NB, NT, 128, 512], BF16, kind=I)
        self.s_att = dr("s_att", [NB, NT, 128, 512], BF16, kind=I)

        self.pb = [self.psb(f"pb{i}") for i in range(8)]
        cf = self.sb("cf", [128, 7 * 128])
        self.cf = cf
        self.dma(cf[:], self.cf32[:, :], W=[cf])
        self.ident_f = cf[:, 0:128]
        self.cU1, self.cL1 = cf[:, 384:512], cf[:, 512:640]
        self.cU16, self.cL16 = cf[:, 640:768], cf[:, 768:896]
        cb = self.sb("cb", [128, 4 * 128], BF16)
        self.cb = cb
        self.cp(self.dve, cb[:, 0:384], cf[:, 0:384], [cf], [cb])
        self.mset(self.dve, cb[:, 384:512], 1.0, [cb])
        self.ident_b = cb[:, 0:128]
        self.mU, self.mL = cb[:, 128:256], cb[:, 256:384]
        self.ones_b = cb[:, 384:512]
        cn = self.sb("cn", [128, 128])
        self.cn = cn
        self.mset(self.dve, cn[:, 0:64], -1.0, [cn])
        self.mset(self.dve, cn[:, 64:128], -1.0 / 16.0, [cn])

        cst = self.sb("cst", [128, 4])
        self.cst = cst
        self.mset(self.dve, cst[:, 0:1], EPS, [cst])
        self.mset(self.dve, cst[:, 1:2], 1.0, [cst])
        self.mset(self.dve, cst[:, 2:3], LN8, [cst])
        self.epsb = cst
        self.c_eps, self.c_one, self.c_ln8 = cst[:, 0:1], cst[:, 1:2], cst[:, 2:3]
        import os
        stop = os.environ.get("MK_STOP", "")
        self.modulation()
        for l in range(L):
            if stop == "mod":
                break
            self.phase1(l)
            if stop == "p1":
                break
            self.phase3a(l)
            if stop == "p3a":
                break
            self.phase3b(l)
            if stop == "p3b":
                break
        self.barrier()

    def modulation(self):
        nc = self.nc
        L, NG = self.DEPTH, self.NG
        self.mod = self.sb("mod", [128, L, 48, NG])
        self.A1 = self.sb("A1", [128, L, KC, NG])
        self.A2 = self.sb("A2", [128, L, KC, NG])
        with ExitStack() as st:
            cTs = self.sb("cTs", [128, KC, NG], stack=st)
            sc = self.sb("scT", [128, KC, NG], stack=st)
            tmp = self.sb("modtmp", [128, KC, NG], stack=st)
            stg = [self.sb(f"adastg{i}", [128, KC, 512], stack=st) for i in range(2)]
            bad = self.sb("bad", [128, L, 48], stack=st)
            ng = self.sb("ng", [128, 2, L, KC], stack=st)
            self.dma(cTs[:], self.cT[:, :, :], W=[cTs])
            self.dma(bad[:], self.b_adaT.rearrange("l p c -> p l c"), W=[bad])
            self.dma(ng[:, 0], self.n1g.rearrange("l p c -> p l c"), W=[ng])
            self.dma(ng[:, 1], self.n2g.rearrange("l p c -> p l c"), W=[ng])
            self.actf(tmp[:], cTs[:], AF.Exp, [cTs], [tmp], scale=-1.0)
            self.ts(self.dve, tmp[:], tmp[:], 1.0, ALU.add, [tmp], [tmp])
            self.recip(tmp[:], tmp[:], [tmp], [tmp])
            self.tt(self.dve, sc[:], cTs[:], tmp[:], ALU.mult, [cTs, tmp], [sc])
            psm = self.pb[0]
            i = 0
            for l in range(L):
                pv = psm[:, 0:48 * NG].rearrange("p (c g) -> p c g", g=NG)
                for cg in range(12):
                    s = stg[i % 2]
                    i += 1
                    self.dma(s[:], self.w_ada[l, :, cg * 512:(cg + 1) * 512].rearrange("(kc p) c -> p kc c", p=128),
                             W=[s])
                    for j in range(4):
                        for kc in range(KC):
                            self.mm(pv[:, cg * 4 + j, :], s[:, kc, j * 128:(j + 1) * 128], sc[:, kc, :],
                                    kc == 0, kc == KC - 1, [s, sc], [psm])
                self.tt(self.dve, self.mod[:, l], pv, bad[:, l].unsqueeze(2).broadcast_to([128, 48, NG]), ALU.add,
                        [psm, bad], [self.mod])
                for (A, which, gi) in ((self.A1, 1, 0), (self.A2, 4, 1)):
                    self.stt(A[:, l], self.mod[:, l, which * 8:(which + 1) * 8, :], 1.0,
                             ng[:, gi, l].unsqueeze(2).broadcast_to([128, KC, NG]), ALU.add, ALU.mult,
                             [self.mod, ng], [A])
            self.barrier()

    def modv(self, l, which, c, g):
        return self.mod[:, l, which * 8 + c, g:g + 1]

    def load_weight(self, dst, src_ap, nk, ncols_list, stg, R=()):
        for (s0, s1, d0) in ncols_list:
            for k0 in range(0, nk, KC):
                for c0 in range(s0, s1, 512):
                    c1 = min(c0 + 512, s1)
                    w = c1 - c0
                    s = stg[self.rr % 2]
                    self.dma(s[:, :, 0:w], src_ap[k0 * 128:(k0 + KC) * 128, c0:c1].rearrange("(kc p) c -> p kc c", p=128),
                             W=[s])
                    eng = self.rot([self.dve, self.pool, self.act])
                    dd = d0 + (c0 - s0)
                    self.cp(eng, dst[:, k0:k0 + KC, dd:dd + w], s[:, :, 0:w], [s], [dst])

    def norm_block(self, st, xblk, W_, A, Bv, out_bf, pbn, names):
        sq, rstd, tmpn = names
        self.tt(self.pool, sq[:, :, 0:W_], xblk[:, :, 0:W_], xblk[:, :, 0:W_], ALU.mult, [xblk], [sq])
        for c in range(KC):
            self.mm(pbn[:, 0:W_], self.ones_b, sq[:, c, 0:W_], c == 0, c == KC - 1, [sq, self.cb], [pbn])
        self.actf(rstd[:, 0:W_], pbn[:, 0:W_], AF.Ln, [pbn], [rstd], scale=1.0 / D, bias=self.c_eps)
        self.actf(rstd[:, 0:W_], rstd[:, 0:W_], AF.Exp, [rstd], [rstd], scale=-0.5)
        if out_bf is None:
            return
        for c in range(KC):
            self.tt(self.dve, tmpn[:, 0:W_], xblk[:, c, 0:W_], rstd[:, 0:W_], ALU.mult, [xblk, rstd], [tmpn])
            self.actf(out_bf[:, c, 0:W_], tmpn[:, 0:W_], AF.Identity, [tmpn, self.mod, self.A1, self.A2], [out_bf],
                      scale=A(c), bias=Bv(c))

    def phase1(self, l):
        nc = self.nc
        NB, NT, NTL, NTC, T, NG = self.NB, self.NT, self.NTL, self.NTC, self.T, self.NG
        last = (l == self.DEPTH - 1)
        pb = self.pb
        with ExitStack() as st:
            sb = lambda name, shape, dt=F32: self.sb(f"p1_{name}", shape, dt, stack=st)
            w_in = sb("w_in", [128, KC, INC], BF16)
            with ExitStack() as st2:
                stg = [self.sb(f"p1_stg{i}", [128, KC, 512], stack=st2) for i in range(2)]
                self.load_weight(w_in, self.w_in[l], KC, WIN_MAP, stg)
                self.barrier()
            self.ck("w")
            waug = sb("waug", [33, 512])
            self.dma(waug[:], self.waug[l], W=[waug])
            bi = sb("bi", [128, 8])
            bfz = sb("bfz", [128, 8])
            esink = sb("esink", [128, 8])
            self.dma(bi[:], self.mib[l].partition_broadcast(128), W=[bi])
            self.dma(bfz[:], self.mfb[l].partition_broadcast(128), W=[bfz])
            self.dma(esink[:], self.sink[l].partition_broadcast(128), W=[esink])
            self.actf(esink[:], esink[:], AF.Exp, [esink], [esink])
            xblk = [sb(f"xblk{i}", [128, KC, 256]) for i in range(2)]
            sq = sb("sq", [128, KC, 256], BF16)
            xhat = [sb(f"xhat{i}", [128, KC, 256], BF16) for i in range(2)]
            pr = [sb(f"pr{i}", [128, INC]) for i in range(2)]
            rstd = sb("rstd", [128, 256])
            tmpn = sb("tmpn", [128, 256])
            KT = sb("KT", [64, 2, T], BF16)
            ktok = [Tok(f"kt{t}") for t in range(NT)]
            VA = sb("VA", [128, NT, 2, 65], BF16)
            vtok = [Tok(f"vt{t}") for t in range(NT)]
            ropet = [sb(f"ropet{i}", [128, 4, 64]) for i in range(2)]
            t1 = sb("t1", [128, 512])
            t2 = sb("t2", [128, 512])
            q_r = sb("q_r", [128, 8, 64], BF16)
            k_r = sb("k_r", [128, 2, 64], BF16)
            t1k = sb("t1k", [128, 128])
            t2k = sb("t2k", [128, 128])
            qT = [sb(f"qT{i}", [64, 8, 128], BF16) for i in range(3)]
            logi = sb("logi", [128, 8])
            lm = sb("lm", [128, 8])
            glrT = sb("glrT", [33, 128])
            self.mset(self.dve, glrT[:], 1.0, [glrT])
            lg = sb("lg", [128, 512])
            ebm = sb("ebm", [128, 8])
            ksm = sb("ksm", [128, 8])
            ebg = sb("ebg", [128, 512])
            enbg = sb("enbg", [128, 512])
            dec = sb("dec", [64, 16])
            qp = [sb(f"qp{d}", [128, 8, 64], BF16) for d in range(2)]
            kp = [sb(f"kp{d}", [128, 8, 64], BF16) for d in range(2)]
            qpT = [sb(f"qpT{d}", [64, 8, 128], BF16) for d in range(2)]
            kpT = [sb(f"kpT{d}", [64, 8, 128], BF16) for d in range(2)]
            AT = [sb(f"AT{d}", [128, 8, 128], BF16) for d in range(2)]
            vaug = sb("vaug", [128, 8, 65], BF16)
            self.mset(self.dve, vaug[:], 1.0, [vaug])
            gates = sb("gates", [128, 512], BF16)
            gt1 = sb("gt1", [128, 256])
            gt2 = sb("gt2", [128, 256])
            P = [sb(f"P{i}", [128, 4, 128], BF16) for i in range(5)]
            hf = sb("hf", [128, 8, 64])
            oin = sb("oin", [128, 8, 65])
            usb = sb("usb", [64, 8, 65])
            Sst = sb("S", [64, 8, 65])
            Sbf = sb("Sbf", [64, 8, 65], BF16)
            stmp = sb("stmp", [64, 8, 65])
            dn = sb("dn", [128, 8, 1])
            ao = sb("ao", [128, 8, 64], BF16)
            aoT = sb("aoT", [128, 4, 128], BF16)
            self.mset(self.pool, VA[:], 1.0, [VA] + vtok)

            pbT = pb[2]
            pbTv = pbT[:].bitcast(BF16)

            dna = sb("dna", [128, 8, 1])

            def attention(b, tq, qTt, keys):
                pS, pv = pb[6], pb[7]
                for g in range(2):
                    for ci, (kt, msk) in enumerate(keys):
                        self.mm(pS[:, 0:512], KT[:, g, kt * 128:(kt + 1) * 128], qTt[:, 4 * g:4 * g + 4, :],
                                True, True, [ktok[kt], qTt], [pS])
                        Pv = P[ci][:].rearrange("p h q -> p (h q)")
                        self.actf(Pv, pS[:, 0:512], AF.Exp, [pS], [P[ci]])
                        if msk is not None:
                            self.tt(self.pool, P[ci][:], P[ci][:], msk.unsqueeze(1).broadcast_to([128, 4, 128]),
                                    ALU.mult, [P[ci], self.cb], [P[ci]])
                        yield
                    pvv = pv[:, 0:260].rearrange("p (h e) -> p h e", e=65)
                    for hg in range(4):
                        for ci, (kt, msk) in enumerate(keys):
                            self.mm(pvv[:, hg, :], P[ci][:, hg, :], VA[:, kt, g, :], ci == 0, ci == len(keys) - 1,
                                    [P[ci], vtok[kt]], [pv])
                        yield
                    self.tt(self.dve, dna[:, 4 * g:4 * g + 4, :], pvv[:, :, 64:65], esink[:, 4 * g:4 * g + 4].unsqueeze(2),
                            ALU.add, [pv, esink], [dna])
                    self.recip(dna[:, 4 * g:4 * g + 4, :], dna[:, 4 * g:4 * g + 4, :], [dna], [dna])
                    self.tt(self.dve, ao[:, 4 * g:4 * g + 4, :], pvv[:, :, 0:64],
                            dna[:, 4 * g:4 * g + 4, :].broadcast_to([128, 4, 64]), ALU.mult, [pv, dna], [ao])
                    yield
                aov = ao[:].rearrange("p h d -> p (h d)")
                pTa = pbTv[:, 0:512].rearrange("p (k t) -> p k t", t=128)
                for k in range(4):
                    self.tr(pTa[:, k, :], aov[:, k * 128:(k + 1) * 128], self.ident_b, [ao, self.cb], [pbT])
                self.cp(self.act, aoT[:], pTa, [pbT], [aoT])
                yield
                self.dma(self.s_att[b, tq].rearrange("p (k t) -> p k t", t=128), aoT[:], R=[aoT],
                         W=[self.dt("att", b, tq)])
                yield

            groups = [GA, GB, GC, GD, GE, GF]

            def load_x(b, blk):
                xb = xblk[blk % 2]
                src = self.xT0 if l == 0 else self.xres
                self.dma(xb[:], src[b, :, :, blk * 256:(blk + 1) * 256].rearrange("c p t -> p c t"),
                         R=[self.dt("x", b, blk)], W=[xb])

            def do_norm(b, blk):
                isctx = (blk * 2 < NTC)
                g = NB if isctx else b
                self.norm_block(st, xblk[blk % 2], 256, lambda c: self.A1[:, l, c, g:g + 1],
                                lambda c: self.modv(l, 0, c, g), xhat[blk % 2], pb[4], (sq, rstd, tmpn))

            def proj_thread(b, t):
                blk, j = t // 2, t % 2
                if j == 0:
                    if blk + 1 < NT // 2:
                        load_x(b, blk + 1)
                    do_norm(b, blk)
                    yield
                xh = xhat[blk % 2][:, :, j * 128:(j + 1) * 128]
                prj = pr[t % 2]
                for gi, G in enumerate(groups):
                    pbk = pb[self.npj % 2]
                    self.npj += 1
                    for kc in range(KC):
                        self.mm(pbk[:, 0:G[1] - G[0]], xh[:, kc, :], w_in[:, kc, G[0]:G[1]], kc == 0,
                                kc == KC - 1, [xhat[blk % 2], w_in], [pbk])
                    yield
                    ev = self.act if gi % 3 != 2 else self.dve
                    self.cp(ev, prj[:, G[0]:G[1]], pbk[:, 0:G[1] - G[0]], [pbk], [prj])
                    yield

            def post(b, t):
                prj = pr[t % 2]
                isctx = (t < NTC)
                tl = t - NTC
                qTt = qT[t % 3]
                if not isctx:
                    rp = ropet[t % 2]
                    self.dma(rp[:], self.rope[tl], W=[rp])

                def rope_apply(c0, H, tab, o1, o2, outv):
                    psv = prj[:, c0:c0 + H * 64]
                    p3 = psv.rearrange("p (h d) -> p h d", d=64)
                    p5 = psv.rearrange("p (h a f j) -> p h a f j", a=2, f=2, j=16)
                    cs = rp[:, tab[0], :].unsqueeze(1).broadcast_to([128, H, 64])
                    sn = rp[:, tab[1], :].rearrange("p (a f j) -> p a f j", a=2, f=2)
                    o1v = o1[:, 0:H * 64].rearrange("p (h d) -> p h d", d=64)
                    o25 = o2[:, 0:H * 64].rearrange("p (h a f j) -> p h a f j", a=2, f=2, j=16)
                    self.tt(self.dve, o1v, p3, cs, ALU.mult, [prj, rp], [o1])
                    for f in range(2):
                        self.tt(self.pool, o25[:, :, :, f, :], p5[:, :, :, 1 - f, :],
                                sn[:, :, f, :].unsqueeze(1).broadcast_to([128, H, 2, 16]), ALU.mult,
                                [prj, rp], [o2])
                    self.tt(self.dve, outv[:].rearrange("p h d -> p (h d)"), o1[:, 0:H * 64], o2[:, 0:H * 64],
                            ALU.add, [o1, o2], [outv])

                self.actf(gt1[:], prj[:, 1584:1840], AF.Exp, [prj], [gt1], scale=-1.0)
                self.actf(gt2[:], prj[:, 2608:2864], AF.Exp, [prj], [gt2], scale=-1.0)
                self.actf(gt1[:], gt1[:], AF.Ln, [gt1], [gt1], bias=self.c_one)
                self.actf(gt2[:], gt2[:], AF.Ln, [gt2], [gt2], bias=self.c_one)
                yield
                if isctx:
                    self.actf(q_r[:].rearrange("p h d -> p (h d)"), prj[:, 0:512], AF.Copy, [prj], [q_r], scale=0.125)
                    self.cp(self.pool, k_r[:].rearrange("p h d -> p (h d)"), prj[:, 512:640], [prj], [k_r])
                else:
                    rope_apply(0, 8, (0, 1), t1, t2, q_r)
                    yield
                    rope_apply(512, 2, (2, 3), t1k, t2k, k_r)
                yield
                self.actf(gates[:, 0:256], gt1[:], AF.Exp, [gt1], [gates], scale=-1.0)
                self.actf(gt2[:], gt2[:], AF.Exp, [gt2], [gt2], scale=-1.0)
                pT8 = pbTv[0:64, :].rearrange("p (h t) -> p h t", t=128)
                for h in range(8):
                    self.tr(pT8[:, h, :], q_r[:, h, :], self.ident_b, [q_r, self.cb], [pbT])
                self.cp(self.act, qTt[:], pT8, [pbT], [qTt])
                yield
                self.cp(self.pool, VA[:, t, :, 0:64], prj[:, 640:768].rearrange("p (h d) -> p h d", d=64),
                        [prj], [vtok[t]])
                self.tt(self.dve, logi[:], prj[:, 768:776], bi[:], ALU.add, [prj, bi], [logi])
                self.tt(self.dve, lm[:], prj[:, 776:784], bfz[:], ALU.add, [prj, bfz], [lm])
                self.tt(self.dve, gates[:, 256:512], prj[:, 2608:2864], gt2[:], ALU.mult, [prj, gt2], [gates])
                pT2 = pbTv[0:64, 0:256].rearrange("p (h t) -> p h t", t=128)
                for h in range(2):
                    self.tr(pT2[:, h, :], k_r[:, h, :], self.ident_b, [k_r, self.cb], [pbT])
                self.cp(self.act, KT[:, :, t * 128:(t + 1) * 128], pT2, [pbT], [ktok[t]])
                self.dma(self.s_gate[b, t], gates[:], R=[gates], W=[self.dt("gate", b, t)])
                yield
                self.actf(lm[:], lm[:], AF.Exp, [lm], [lm], scale=-1.0)
                self.actf(lm[:], lm[:], AF.Ln, [lm], [lm], bias=self.c_one)
                self.tr(pb[4][0:32, 128:256], prj[:, 784:816], self.ident_f, [prj, self.cf], [pb[4]])
                self.cp(self.dve, glrT[0:32, :], pb[4][0:32, 128:256], [pb[4]], [glrT])
                yield
                self.mm(pb[3][:, 0:512], glrT[:], waug[:], True, True, [glrT, waug], [pb[3]])
                self.actf(lg[:], pb[3][:, 0:512], AF.Exp, [pb[3]], [lg], scale=-1.0)
                self.actf(lg[:], lg[:], AF.Ln, [lg], [lg], bias=self.c_one)
                yield
                self.mm(pb[4][:, 0:4], self.cU1, lm[:, 0:4], True, True, [lm, self.cf], [pb[4]])
                self.mm(pb[4][:, 4:8], self.cL1, lm[:, 4:8], True, True, [lm, self.cf], [pb[4]])
                self.mm(pb[3][:, 0:256], self.cU16, lg[:, 0:256], True, True, [lg, self.cf], [pb[3]])
                self.mm(pb[3][:, 256:512], self.cL16, lg[:, 256:512], True, True, [lg, self.cf], [pb[3]])
                pdec = pb[4][0:64, 16:32].rearrange("p (d e) -> p d e", e=8)
                for d_ in range(2):
                    self.mm(pdec[:, d_, 0:4], self.cn[:, 0:64], lm[:, 4 * d_:4 * d_ + 4], True, True,
                            [lm, self.cn], [pb[4]])
                    for h in range(4):
                        self.mm(pdec[:, d_, 4 + h:5 + h], lg[:, d_ * 256 + h * 64:d_ * 256 + (h + 1) * 64],
                                self.cn[:, 64:65], True, True, [lg, self.cn], [pb[4]])
                yield
                self.actf(ebg[:], pb[3][:, 0:512], AF.Exp, [pb[3]], [ebg])
                self.actf(enbg[:], pb[3][:, 0:512], AF.Exp, [pb[3]], [enbg], scale=-1.0)
                self.actf(dec[:], pb[4][0:64, 16:32], AF.Exp, [pb[4]], [dec])
                self.actf(ebm[:], pb[4][:, 0:8], AF.Exp, [pb[4]], [ebm])
                self.actf(ksm[:], pb[4][:, 0:8], AF.Copy, [pb[4]], [ksm], scale=-1.0)
                self.tt(self.dve, ksm[:], ksm[:], logi[:], ALU.add, [logi, ksm], [ksm])
                self.actf(ksm[:], ksm[:], AF.Exp, [ksm], [ksm], bias=self.c_ln8)
                yield
                self.cp(self.act, vaug[:, 0:4, 0:64], prj[:, 1328:1584].rearrange("p (h d) -> p h d", d=64),
                        [prj], [vaug])
                self.cp(self.act, vaug[:, 4:8, 0:64], prj[:, 2352:2608].rearrange("p (h d) -> p h d", d=64),
                        [prj], [vaug])
                for d_ in range(2):
                    self.tt(self.dve, qp[d_][:, 0:4, :], prj[:, 816:1072].rearrange("p (h d) -> p h d", d=64),
                            ebm[:, 4 * d_:4 * d_ + 4].unsqueeze(2).broadcast_to([128, 4, 64]), ALU.mult,
                            [prj, ebm], [qp[d_]])
                    self.tt(self.pool, kp[d_][:, 0:4, :], prj[:, 1072:1328].rearrange("p (h d) -> p h d", d=64),
                            ksm[:, 4 * d_:4 * d_ + 4].unsqueeze(2).broadcast_to([128, 4, 64]), ALU.mult,
                            [prj, ksm], [kp[d_]])
                    self.stt(qp[d_][:, 4:8, :].rearrange("p h d -> p (h d)"), prj[:, 1840:2096], 0.125,
                             ebg[:, d_ * 256:(d_ + 1) * 256], ALU.mult, ALU.mult, [prj, ebg], [qp[d_]])
                    self.tt(self.pool, kp[d_][:, 4:8, :].rearrange("p h d -> p (h d)"), prj[:, 2096:2352],
                            enbg[:, d_ * 256:(d_ + 1) * 256], ALU.mult, [prj, enbg], [kp[d_]])
                    yield
                for d_ in range(2):
                    pq = pbTv[0:64, :].rearrange("p (h t) -> p h t", t=128)
                    for h in range(8):
                        self.tr(pq[:, h, :], qp[d_][:, h, :], self.ident_b, [qp[d_], self.cb], [pbT])
                    self.cp(self.act, qpT[d_][:], pq, [pbT], [qpT[d_]])
                    yield
                    for h in range(8):
                        self.tr(pq[:, h, :], kp[d_][:, h, :], self.ident_b, [kp[d_], self.cb], [pbT])
                    self.cp(self.dve, kpT[d_][:], pq, [pbT], [kpT[d_]])
                    yield
                    msk = self.mU if d_ == 0 else self.mL
                    for half in range(2):
                        pA = pb[5]
                        pAv = pA[:, 0:512].rearrange("p (h r) -> p h r", r=128)
                        for hh in range(4):
                            h = half * 4 + hh
                            self.mm(pAv[:, hh, :], kpT[d_][:, h, :], qpT[d_][:, h, :], True, True,
                                    [kpT[d_], qpT[d_]], [pA])
                        self.tt(self.dve, AT[d_][:, half * 4:half * 4 + 4, :], pAv,
                                msk.unsqueeze(1).broadcast_to([128, 4, 128]), ALU.mult, [pA, self.cb], [AT[d_]])
                        yield
                    for half in range(2):
                        pO = pb[3]
                        pOv = pO[:, 0:260].rearrange("p (h e) -> p h e", e=65)
                        pU = pb[4]
                        pUv = pU[0:64, 0:260].rearrange("p (h e) -> p h e", e=65)
                        for hh in range(4):
                            h = half * 4 + hh
                            self.mm(pOv[:, hh, :], AT[d_][:, h, :], vaug[:, h, :], True, d_ == 1,
                                    [AT[d_], vaug], [pO])
                            if d_ == 0:
                                self.mm(pOv[:, hh, :], qpT[0][:, h, :], Sbf[:, h, :], False, True,
                                        [qpT[0], Sbf], [pO])
                        for hh in range(4):
                            h = half * 4 + hh
                            self.mm(pUv[:, hh, :], kp[d_][:, h, :], vaug[:, h, :], True, True,
                                    [kp[d_], vaug], [pU])
                        yield
                        hs = slice(half * 4, half * 4 + 4)
                        if d_ == 0:
                            if half == 0:
                                self.ts(self.dve, dn[:, 0:4, :], pOv[:, :, 64:65], -1.0, ALU.mult, [pO], [dn],
                                        s2=1.0, op1=ALU.max)
                                self.tt(self.dve, dn[:, 0:4, :], dn[:, 0:4, :], pOv[:, :, 64:65], ALU.max,
                                        [pO, dn], [dn])
                                self.recip(dn[:, 0:4, :], dn[:, 0:4, :], [dn], [dn])
                                self.tt(self.dve, hf[:, 0:4, :], pOv[:, :, 0:64],
                                        dn[:, 0:4, :].broadcast_to([128, 4, 64]), ALU.mult, [pO, dn], [hf])
                            else:
                                self.cp(self.act, hf[:, 4:8, :], pOv[:, :, 0:64], [pO], [hf])
                            self.tt(self.dve, stmp[:, hs, :], pUv, Sst[:, hs, :], ALU.add, [pU, Sst], [stmp])
                            self.tt(self.pool, Sst[:, hs, :], stmp[:, hs, :],
                                    dec[:, half * 4:half * 4 + 4].unsqueeze(2).broadcast_to([64, 4, 65]),
                                    ALU.mult, [stmp, dec], [Sst])
                            self.cp(self.act, Sbf[:, hs, :], Sst[:, hs, :], [Sst], [Sbf])
                        else:
                            self.cp(self.act, oin[:, hs, :], pOv, [pO], [oin])
                            self.cp(self.dve, usb[:, hs, :], pUv, [pU], [usb])
                        yield
                    if d_ == 0:
                        self.dma(self.s_hf[b, t], hf[:].rearrange("p h d -> p (h d)"), R=[hf],
                                 W=[self.dt("hf", b, t)])
                    else:
                        self.dma(self.s_oin[b, t], oin[:].rearrange("p h e -> p (h e)"), R=[oin],
                                 W=[self.dt("oin", b, t)])
                        self.dma(self.s_u[b, t], usb[:].rearrange("p h e -> p (h e)"), R=[usb],
                                 W=[self.dt("u", b, t)])
                        self.dma(self.s_dec[b, t], dec[:, 8:16], R=[dec], W=[self.dt("dec", b, t)])
                        self.dma(self.s_qT[b, t], qpT[1][:].rearrange("p h t -> p (h t)"), R=[qpT[1]],
                                 W=[self.dt("qT", b, t)])
                    yield

            ctxkeys = [(c, None) for c in range(NTC)]

            def lat_keys(tq):
                ks = []
                if tq - 1 >= NTC:
                    ks.append((tq - 1, self.mL))
                ks.append((tq, None))
                if tq + 1 < NT:
                    ks.append((tq + 1, self.mU))
                return ks + ctxkeys

            def att_jobs(b, p):
                if p == NTC and not last:
                    for tq in range(NTC):
                        yield from attention(b, tq, qT[tq % 3], ctxkeys)
                tq = p - 2
                if tq >= NTC and tq < NT:
                    yield from attention(b, tq, qT[tq % 3], lat_keys(tq))

            self.npj = 0
            for b in range(NB):
                self.mset(self.dve, Sst[:], 0.0, [Sst])
                self.mset(self.dve, Sbf[:], 0.0, [Sbf])
                load_x(b, 0)
                for t in range(NT + 3):
                    p = t - 1
                    threads = []
                    if t < NT:
                        threads.append(proj_thread(b, t))
                    if 0 <= p < NT:
                        threads.append(post(b, p))
                    if p >= NTC:
                        threads.append(att_jobs(b, p))
                    while threads:
                        for th in list(threads):
                            try:
                                next(th)
                            except StopIteration:
                                threads.remove(th)
            self.barrier()

    def phase3a(self, l):
        nc = self.nc
        NB, NT, NTL, NTC, T, NG = self.NB, self.NT, self.NTL, self.NTC, self.T, self.NG
        last = (l == self.DEPTH - 1)
        pb = self.pb
        with ExitStack() as st:
            sb = lambda name, shape, dt=F32: self.sb(f"p3_{name}", shape, dt, stack=st)
            w_out = sb("w_out", [128, KC, D], BF16)
            with ExitStack() as st2:
                stg = [self.sb(f"p3_stg{i}", [128, KC, 512], stack=st2) for i in range(2)]
                self.load_weight(w_out, self.w_out[l], KC, [(0, D, 0)], stg)
                self.barrier()
            mgn = sb("mgn", [128, 512])
            self.dma(mgn[:], self.mgn[l].partition_broadcast(128), W=[mgn])
            xblk = [sb(f"xblk{i}", [128, KC, 256]) for i in range(2)]
            oT = [sb(f"oT{i}", [128, KC, 256], BF16) for i in range(2)]
            U = [sb(f"U{i}", [64, 8, 65]) for i in range(2)]
            dec = [sb(f"dec{i}", [64, 8]) for i in range(2)]
            qTl = [sb(f"qTl{i}", [64, 8, 128], BF16) for i in range(2)]
            oin = [sb(f"oin{i}", [128, 8, 65]) for i in range(2)]
            hf = [sb(f"hf{i}", [128, 8, 64]) for i in range(2)]
            gat = [sb(f"gat{i}", [128, 512], BF16) for i in range(2)]
            orev = sb("orev", [128, 8, 65])
            hs = sb("hs", [128, 8, 64])
            hsq = sb("hsq", [128, 8, 64])
            ss = sb("ss", [128, 8])
            dn = sb("dn", [128, 4, 1])
            o1 = sb("o1", [128, 8, 64])
            omg = sb("omg", [128, 512], BF16)
            Sst = sb("S", [64, 8, 65])
            Sbf = sb("Sbf", [64, 8, 65], BF16)
            stmp = sb("stmp", [64, 8, 65])
            pbT = pb[2]
            pbTv = pbT[:].bitcast(BF16)
            pr = 0
            tiles = []
            for b in range(NB):
                order = list(range(NTC // 2 - 1, -1, -1)) + list(range(NT // 2 - 1, NTC // 2 - 1, -1))
                for blk in order:
                    for j in (1, 0):
                        tiles.append((b, blk, j, len(tiles) // 2))

            def loads(k):
                b, blk, j, bi_ = tiles[k]
                isctx = (blk * 2 < NTC)
                full = not (isctx and last)
                t = blk * 2 + j
                i2 = k % 2
                if full and j == 1:
                    src = self.xT0 if l == 0 else self.xres
                    self.dma(xblk[bi_ % 2][:], src[b, :, :, blk * 256:(blk + 1) * 256].rearrange("c p t -> p c t"),
                             R=[self.dt("x", b, blk)], W=[xblk[bi_ % 2]])
                self.dma(U[i2][:].rearrange("p h e -> p (h e)"), self.s_u[b, t], R=[self.dt("u", b, t)], W=[U[i2]])
                self.dma(dec[i2][:], self.s_dec[b, t], R=[self.dt("dec", b, t)], W=[dec[i2]])
                if full:
                    self.dma(qTl[i2][:].rearrange("p h t -> p (h t)"), self.s_qT[b, t], R=[self.dt("qT", b, t)],
                             W=[qTl[i2]])
                    self.dma(oin[i2][:].rearrange("p h e -> p (h e)"), self.s_oin[b, t], R=[self.dt("oin", b, t)],
                             W=[oin[i2]])
                    self.dma(hf[i2][:].rearrange("p h d -> p (h d)"), self.s_hf[b, t], R=[self.dt("hf", b, t)],
                             W=[hf[i2]])
                    self.dma(gat[i2][:], self.s_gate[b, t], R=[self.dt("gate", b, t)], W=[gat[i2]])
                    self.dma(oT[bi_ % 2][:, 0:4, j * 128:(j + 1) * 128],
                             self.s_att[b, t].rearrange("p (k t) -> p k t", t=128),
                             R=[self.dt("att", b, t)], W=[oT[bi_ % 2]])

            loads(0)
            for k, (b, blk, j, bi_) in enumerate(tiles):
                if k + 1 < len(tiles):
                    loads(k + 1)
                isctx = (blk * 2 < NTC)
                g = NB if isctx else b
                full = not (isctx and last)
                xb = xblk[bi_ % 2]
                oTb = oT[bi_ % 2]
                t = blk * 2 + j
                i2 = k % 2
                Ut, dect, qTt, oint, hft, gatt = U[i2], dec[i2], qTl[i2], oin[i2], hf[i2], gat[i2]
                if j == 1 and blk == NTC // 2 - 1:
                    self.mset(self.dve, Sst[:], 0.0, [Sst])
                    self.mset(self.dve, Sbf[:], 0.0, [Sbf])
                if full:
                    for half in range(2):
                        pO = pb[4 + half]
                        pOv = pO[:, 0:260].rearrange("p (h e) -> p h e", e=65)
                        hsl = slice(half * 4, half * 4 + 4)
                        for hh in range(4):
                            h = half * 4 + hh
                            self.mm(pOv[:, hh, :], qTt[:, h, :], Sbf[:, h, :], True, True, [qTt, Sbf], [pO])
                        self.tt(self.dve, orev[:, hsl, :], pOv, oint[:, hsl, :], ALU.add, [pO, oint], [orev])
                self.tt(self.pool, stmp[:], Sst[:], Ut[:], ALU.add, [Sst, Ut], [stmp])
                self.tt(self.pool, Sst[:], stmp[:], dect[:].unsqueeze(2).broadcast_to([64, 8, 65]), ALU.mult,
                        [stmp, dect], [Sst])
                self.cp(self.act, Sbf[:], Sst[:], [Sst], [Sbf])
                if not full:
                    continue
                self.ts(self.dve, dn[:], orev[:, 0:4, 64:65], -1.0, ALU.mult, [orev], [dn], s2=1.0, op1=ALU.max)
                self.tt(self.dve, dn[:], dn[:], orev[:, 0:4, 64:65], ALU.max, [orev, dn], [dn])
                self.recip(dn[:], dn[:], [dn], [dn])
                self.tt(self.dve, hs[:, 0:4, :], orev[:, 0:4, 0:64], dn[:].broadcast_to([128, 4, 64]), ALU.mult,
                        [orev, dn], [hs])
                self.tt(self.pool, hs[:, 0:4, :], hs[:, 0:4, :], hft[:, 0:4, :], ALU.add, [hs, hft], [hs])
                self.tt(self.pool, hs[:, 4:8, :], orev[:, 4:8, 0:64], hft[:, 4:8, :], ALU.add, [orev, hft], [hs])
                self.tt(self.pool, hsq[:], hs[:], hs[:], ALU.mult, [hs], [hsq])
                self.op(self.dve, lambda: nc.vector.tensor_reduce(out=ss[:], in_=hsq[:], axis=AX.X, op=ALU.add),
                        [hsq], [ss])
                self.actf(ss[:], ss[:], AF.Ln, [ss], [ss], scale=1.0 / HD, bias=self.c_eps)
                self.actf(ss[:], ss[:], AF.Exp, [ss], [ss], scale=-0.5)
                self.tt(self.dve, o1[:], hs[:], ss[:].unsqueeze(2).broadcast_to([128, 8, 64]), ALU.mult,
                        [hs, ss], [o1])
                o1v = o1[:].rearrange("p h d -> p (h d)")
                self.tt(self.pool, o1v, o1v, mgn[:], ALU.mult, [o1, mgn], [o1])
                self.tt(self.dve, omg[:], o1v, gatt[:], ALU.mult, [o1, gatt], [omg])
                pTa = pbTv[:, 0:512].rearrange("p (k t) -> p k t", t=128)
                for kk in range(4):
                    self.tr(pTa[:, kk, :], omg[:, kk * 128:(kk + 1) * 128], self.ident_b, [omg, self.cb], [pbT])
                self.cp(self.act, oTb[:, 4:8, j * 128:(j + 1) * 128], pTa, [pbT], [oTb])
                if j != 0:
                    continue
                for oc in range(KC):
                    pk = pb[pr % 2]
                    pr += 1
                    for kc in range(KC):
                        self.mm(pk[:, 0:256], w_out[:, kc, oc * 128:(oc + 1) * 128], oTb[:, kc, :], kc == 0,
                                kc == KC - 1, [w_out, oTb], [pk])
                    self.stt(xb[:, oc, :], pk[:, 0:256], self.modv(l, 2, oc, g), xb[:, oc, :], ALU.mult, ALU.add,
                             [pk, xb, self.mod], [xb])
                self.dma(self.xres[b, :, :, blk * 256:(blk + 1) * 256].rearrange("c p t -> p c t"), xb[:],
                         R=[xb], W=[self.dt("x", b, blk)])
            self.barrier()

    def phase3b(self, l):
        nc = self.nc
        NB, NT, NTL, NTC, T, NG = self.NB, self.NT, self.NTL, self.NTC, self.T, self.NG
        last = (l == self.DEPTH - 1)
        pb = self.pb
        with ExitStack() as st:
            sb = lambda name, shape, dt=F32: self.sb(f"p4_{name}", shape, dt, stack=st)
            w1 = sb("w1", [128, KC, DFF], BF16)
            w2 = sb("w2", [128, HC, D], BF16)
            with ExitStack() as st2:
                stg = [self.sb(f"p4_stg{i}", [128, KC, 512], stack=st2) for i in range(2)]
                self.load_weight(w1, self.w_mlp1[l], KC, [(0, DFF, 0)], stg)
                self.load_weight(w2, self.w_mlp2[l], HC, [(0, D, 0)], stg)
                self.barrier()
            fg = sb("fg", [128, KC])
            self.dma(fg[:], self.fing[:, :], W=[fg])
            xblk = [sb(f"xblk{i}", [128, KC, 256]) for i in range(2)]
            hT = [sb(f"hT{i}", [128, KC, 256], BF16) for i in range(2)]
            sq = sb("sq", [128, KC, 256], BF16)
            hid = sb("hid", [128, HC, 256], BF16)
            rstd = sb("rstd", [128, 256])
            tmpn = sb("tmpn", [128, 256])
            rl = [sb(f"rl{i}", [128, 256]) for i in range(4)]
            yout = [sb(f"yout{i}", [128, 256]) for i in range(2)]
            pr = 0
            pbs = [pb[0], pb[1], pb[4], pb[5]]
            blocks = []
            for b in range(NB):
                for blk in range(NT // 2):
                    if (blk * 2 < NTC) and last:
                        continue
                    blocks.append((b, blk))

            def load_x(i):
                b, blk = blocks[i]
                self.dma(xblk[i % 2][:], self.xres[b, :, :, blk * 256:(blk + 1) * 256].rearrange("c p t -> p c t"),
                         R=[self.dt("x", b, blk)], W=[xblk[i % 2]])

            def do_norm(i):
                b, blk = blocks[i]
                g = NB if (blk * 2 < NTC) else b
                self.norm_block(st, xblk[i % 2], 256, lambda c: self.A2[:, l, c, g:g + 1],
                                lambda c: self.modv(l, 3, c, g), hT[i % 2], pb[3], (sq, rstd, tmpn))

            load_x(0)
            do_norm(0)
            for i, (b, blk) in enumerate(blocks):
                g = NB if (blk * 2 < NTC) else b
                xb = xblk[i % 2]
                hTb = hT[i % 2]
                if i + 1 < len(blocks):
                    load_x(i + 1)
                for hc in range(HC):
                    pk = pbs[pr % 4]
                    pr += 1
                    for kc in range(KC):
                        self.mm(pk[:, 0:256], w1[:, kc, hc * 128:(hc + 1) * 128], hTb[:, kc, :], kc == 0,
                                kc == KC - 1, [w1, hTb], [pk])
                    r_ = rl[hc % 4]
                    self.actf(r_[:], pk[:, 0:256], AF.Relu, [pk], [r_])
                    self.tt(self.pool, hid[:, hc, :], r_[:], r_[:], ALU.mult, [r_], [hid])
                if i + 1 < len(blocks) and not last:
                    do_norm(i + 1)
                for oc in range(KC):
                    pk = pbs[pr % 4]
                    pr += 1
                    for hc in range(HC):
                        self.mm(pk[:, 0:256], w2[:, hc, oc * 128:(oc + 1) * 128], hid[:, hc, :], hc == 0,
                                hc == HC - 1, [w2, hid], [pk])
                    self.stt(xb[:, oc, :], pk[:, 0:256], self.modv(l, 5, oc, g), xb[:, oc, :], ALU.mult, ALU.add,
                             [pk, xb, self.mod], [xb])
                if not last:
                    self.dma(self.xres[b, :, :, blk * 256:(blk + 1) * 256].rearrange("c p t -> p c t"), xb[:],
                             R=[xb], W=[self.dt("x", b, blk)])
                else:
                    self.norm_block(st, xb, 256, None, None, None, pb[3], (sq, rstd, tmpn))
                    lb = blk - NTC // 2
                    for c in range(KC):
                        y = yout[c % 2]
                        self.stt(y[:], xb[:, c, :], fg[:, c:c + 1], rstd[:], ALU.mult, ALU.mult, [xb, fg, rstd], [y])
                        self.dma(self.outT[b, c, :, lb * 256:(lb + 1) * 256], y[:], R=[y],
                                 W=[self.dt("out", b, blk, c)])
                    if i + 1 < len(blocks):
                        do_norm(i + 1)
            self.barrier()


def host_constants(NTL):
    s = np.arange(128)
    U = (s[:, None] <= s[None, :]).astype(np.float32)
    Lm = (s[:, None] >= s[None, :]).astype(np.float32)
    cf = np.concatenate([np.eye(128, dtype=np.float32), U, Lm, -U, -Lm, -U / 16.0, -Lm / 16.0], axis=1)
    t = np.arange(NTL * 128)
    inv = (10000.0 ** (-np.arange(0, 32, 2, dtype=np.float32) / 32.0)).astype(np.float32)
    ang_r = (t // 64).astype(np.float32)[:, None] * inv[None, :]
    ang_c = (t % 64).astype(np.float32)[:, None] * inv[None, :]
    cr, sr, cc, sc_ = np.cos(ang_r), np.sin(ang_r), np.cos(ang_c), np.sin(ang_c)
    CS = np.concatenate([cr, cr, cc, cc], axis=1).astype(np.float32)
    SN = np.concatenate([-sr, sr, -sc_, sc_], axis=1).astype(np.float32)
    rope = np.stack([CS * 0.125, SN * 0.125, CS, SN], axis=1).reshape(NTL, 128, 4, 64)
    return cf, np.ascontiguousarray(rope.astype(np.float32))


_CACHE = {}


def run(inputs, NB, n_cores, DEPTH):
    x = np.asarray(inputs["x"], dtype=np.float32)
    ctx = np.asarray(inputs["ctx"], dtype=np.float32)
    B, N, _ = x.shape
    NCX = ctx.shape[1]
    NTL, NTC = N // 128, NCX // 128
    T = N + NCX
    key = (NB, NTL, NTC, DEPTH)
    if key not in _CACHE:
        mk = MK(NB, NTL, NTC, DEPTH)
        mk.build()
        _CACHE[key] = mk
    mk = _CACHE[key]
    f = lambda k: np.asarray(inputs[k], dtype=np.float32)
    L = DEPTH
    cf, rope = host_constants(NTL)
    xall = np.concatenate([ctx, x], axis=1)
    xT = np.ascontiguousarray(xall.transpose(0, 2, 1)).reshape(B, KC, 128, T)
    fm = lambda v: np.ascontiguousarray(v.reshape(-1, KC, 128).transpose(0, 2, 1))
    waug = np.zeros((L, 33, 512), np.float32)
    g_wa2, g_ba = f("g_wa2"), f("g_ba")
    waug[:, 0:16, 0:256] = g_wa2[:, 0]
    waug[:, 16:32, 256:512] = g_wa2[:, 1]
    waug[:, 32, 0:256] = g_ba[:, 0]
    waug[:, 32, 256:512] = g_ba[:, 1]
    shared = {
        "w_ada": f("w_ada"), "b_adaT": np.ascontiguousarray(f("b_ada").reshape(L, 48, 128).transpose(0, 2, 1)),
        "n1g": fm(f("norm1_g")), "n2g": fm(f("norm2_g")), "fing": fm(f("final_g")[None])[0],
        "w_in": f("w_in"), "w_out": f("w_out"), "w_mlp1": f("w_mlp1"), "w_mlp2": f("w_mlp2"),
        "sink": f("attn_sink"), "mib": f("m_i_bias").reshape(L, 8), "mfb": f("m_f_bias").reshape(L, 8),
        "mgn": np.concatenate([f("m_norm_g"), f("g_norm_g")], axis=1), "waug": waug,
        "cf32": cf, "rope": rope,
    }
    c, c_ctx = f("c"), f("c_ctx")
    in_maps = []
    for i in range(n_cores):
        bs = slice(i * NB, (i + 1) * NB)
        cc = np.concatenate([c[bs], c_ctx[None]], axis=0)
        cT = np.ascontiguousarray(cc.reshape(NB + 1, KC, 128).transpose(2, 1, 0))
        m = dict(shared)
        m["xT0"] = xT[bs]
        m["cT"] = cT
        in_maps.append(m)
    res = run_bass_kernel_spmd(mk.nc, in_maps, core_ids=list(range(n_cores)))
    outT = np.concatenate([r["outT"] for r in res.results], axis=0)
    return np.ascontiguousarray(outT.reshape(B, D, N).transpose(0, 2, 1))


def kernel(**inputs):
    return run(inputs, NB=2, n_cores=8, DEPTH=4)
```

```python
import numpy as np
from contextlib import ExitStack
import concourse.bass as bass
import concourse.mybir as mybir
from concourse.bass_utils import run_bass_kernel_spmd

F32 = mybir.dt.float32
BF16 = mybir.dt.bfloat16
AF = mybir.ActivationFunctionType
ALU = mybir.AluOpType
AX = mybir.AxisListType

D = 1024
KC = 8
HD = 64
DFF = 4096
HC = 32
INC = 2864
EPS = 1e-6
LN8 = float(np.log(0.125))

WIN_MAP = [(0, 512, 0), (512, 768, 512), (1792, 1808, 768), (2832, 2864, 784),
           (768, 1280, 816), (1280, 1792, 1328), (1808, 2320, 1840), (2320, 2832, 2352)]
GA, GB, GC, GD, GE, GF = (0, 512), (512, 816), (816, 1328), (1328, 1840), (1840, 2352), (2352, 2864)


class Tok:
    __slots__ = ("name", "w", "r", "excl")

    def __init__(self, name):
        self.name = name
        self.w = None
        self.r = {}
        self.excl = False


class TT:
    def __init__(self, t, name):
        self.t = t
        self.tok = Tok(name)
        self.name = name

    def __getitem__(self, idx):
        return self.t[idx]


class Eng:
    def __init__(self, name, h, sem, self_sync=True):
        self.name = name
        self.h = h
        self.sem = sem
        self.cnt = 0
        self.known = {}
        self.self_sync = self_sync


def _tk(x):
    return x.tok if hasattr(x, "tok") else x


class MK:
    def __init__(self, NB, NTL, NTC, DEPTH):
        self.NB, self.NTL, self.NTC, self.DEPTH = NB, NTL, NTC, DEPTH
        self.NT = NTL + NTC
        self.T = self.NT * 128
        self.N = NTL * 128
        self.NG = NB + 1
        self.nc = bass.Bass('TRN2', target_bir_lowering=False)
        self.stack = ExitStack()
        nc = self.nc
        S = self.stack.enter_context
        self.pe = Eng('pe', nc.tensor, S(nc.semaphore("s_pe")), self_sync=False)
        self.act = Eng('act', nc.scalar, S(nc.semaphore("s_act")))
        self.dve = Eng('dve', nc.vector, S(nc.semaphore("s_dve")))
        self.pool = Eng('pool', nc.gpsimd, S(nc.semaphore("s_pool")))
        self.sp = Eng('sp', nc.sync, None)
        self.engs = [self.pe, self.act, self.dve, self.pool]
        S(nc.allow_low_precision("bf16 matmul operands, fp32 accumulation"))
        self.QK = 8
        self.qsems = [S(nc.semaphore(f"s_q{i}")) for i in range(self.QK)]
        self.qcnt = [0] * self.QK
        self.qidx = 0
        self.dtok = {}
        self.rr = 0

    def _waits(self, eng, reads, writes):
        need = {}

        def add(ev):
            if ev is None:
                return
            sem, val, src = ev
            if src is eng and not eng.self_sync:
                return
            k = id(sem)
            if k not in need or need[k][1] < val:
                need[k] = (sem, val)

        for t in reads:
            t = _tk(t)
            add(t.w)
            if t.excl:
                for ev in t.r.values():
                    if ev[2] is not eng:
                        add(ev)
        for t in writes:
            t = _tk(t)
            add(t.w)
            for ev in t.r.values():
                add(ev)
        for k, (sem, val) in need.items():
            if eng.known.get(k, 0) < val:
                eng.h.wait_ge(sem, val)
                eng.known[k] = val

    def _record(self, ev, reads, writes):
        for t in reads:
            _tk(t).r[id(ev[0])] = ev
        for t in writes:
            t = _tk(t)
            t.w = ev
            t.r = {}

    def ck(self, tag):
        import os
        if os.environ.get("MK_STOP", "") == tag:
            self.stopped = True

    def op(self, eng, fn, R=(), W=()):
        if getattr(self, "stopped", False):
            return
        self._waits(eng, R, W)
        inst = fn()
        eng.cnt += 1
        inst.then_inc(eng.sem, 1)
        self._record((eng.sem, eng.cnt, eng), R, W)

    def dma(self, out, in_, R=(), W=()):
        if getattr(self, "stopped", False):
            return
        eng = self.sp
        self._waits(eng, R, W)
        slot = self.qidx % self.QK
        self.qidx += 1
        sem = self.qsems[slot]
        if self.qcnt[slot] > 0 and eng.known.get(id(sem), 0) < self.qcnt[slot]:
            eng.h.wait_ge(sem, self.qcnt[slot])
            eng.known[id(sem)] = self.qcnt[slot]
        inst = eng.h.dma_start(out=out, in_=in_)
        self.qcnt[slot] += 16
        inst.then_inc(sem, 16)
        self._record((sem, self.qcnt[slot], None), R, W)

    def barrier(self):
        evs = [(e.sem, e.cnt) for e in self.engs if e.cnt > 0]
        evs += [(self.qsems[i], self.qcnt[i]) for i in range(self.QK) if self.qcnt[i] > 0]
        for e in self.engs + [self.sp]:
            for sem, val in evs:
                if e.known.get(id(sem), 0) < val:
                    e.h.wait_ge(sem, val)
                    e.known[id(sem)] = val

    def dt(self, *key):
        if key not in self.dtok:
            self.dtok[key] = Tok(str(key))
        return self.dtok[key]

    def sb(self, name, shape, dt=F32, stack=None):
        self.uid = getattr(self, "uid", 0) + 1
        name = f"{name}_{self.uid}"
        t = (stack or self.stack).enter_context(self.nc.sbuf_tensor(name, list(shape), dt))
        return TT(t, name)

    def psb(self, name, stack=None):
        t = (stack or self.stack).enter_context(self.nc.psum_tensor(name, [128, 512], F32))
        r = TT(t, name)
        r.tok.excl = True
        return r

    def mm(self, out, lhsT, rhs, start, stop, R, W):
        self.op(self.pe, lambda: self.nc.tensor.matmul(out, lhsT=lhsT, rhs=rhs, start=start, stop=stop), R, W)

    def tr(self, out, in_, ident, R, W):
        self.op(self.pe, lambda: self.nc.tensor.transpose(out=out, in_=in_, identity=ident), R, W)

    def actf(self, out, in_, func, R, W, scale=None, bias=None):
        kw = {}
        if scale is not None:
            kw["scale"] = scale
        if bias is not None:
            kw["bias"] = bias
        if bias is not None and not isinstance(bias, float):
            R = list(R) + [self.cst]
        self.op(self.act, lambda: self.nc.scalar.activation(out=out, in_=in_, func=func, **kw), R, W)

    def tt(self, eng, out, in0, in1, op, R, W):
        self.op(eng, lambda: eng.h.tensor_tensor(out=out, in0=in0, in1=in1, op=op), R, W)

    def ts(self, eng, out, in0, s1, op0, R, W, s2=None, op1=None):
        if op1 is None:
            self.op(eng, lambda: eng.h.tensor_scalar(out=out, in0=in0, scalar1=s1, scalar2=None, op0=op0), R, W)
        else:
            self.op(eng, lambda: eng.h.tensor_scalar(out=out, in0=in0, scalar1=s1, scalar2=s2, op0=op0, op1=op1), R, W)

    def stt(self, out, in0, scalar, in1, op0, op1, R, W):
        self.op(self.dve, lambda: self.nc.vector.scalar_tensor_tensor(out=out, in0=in0, scalar=scalar, in1=in1,
                                                                     op0=op0, op1=op1), R, W)

    def cp(self, eng, out, in_, R, W):
        if eng is self.act:
            self.op(eng, lambda: self.nc.scalar.copy(out=out, in_=in_), R, W)
        else:
            self.op(eng, lambda: eng.h.tensor_copy(out=out, in_=in_), R, W)

    def recip(self, out, in_, R, W):
        self.op(self.dve, lambda: self.nc.vector.reciprocal(out=out, in_=in_), R, W)

    def mset(self, eng, ap, val, W):
        self.op(eng, lambda: eng.h.memset(ap, val), (), W)

    def rot(self, engs):
        self.rr += 1
        return engs[self.rr % len(engs)]

    def build(self):
        nc = self.nc
        NB, NT, NTL, NTC, T, N, NG, L = self.NB, self.NT, self.NTL, self.NTC, self.T, self.N, self.NG, self.DEPTH
        dr = lambda name, shape, dt=F32, kind="ExternalInput": nc.dram_tensor(name, list(shape), dt, kind=kind).ap()
        self.xT0 = dr("xT0", [NB, KC, 128, T])
        self.cT = dr("cT", [128, KC, NG])
        self.w_ada = dr("w_ada", [L, D, 6 * D])
        self.b_adaT = dr("b_adaT", [L, 128, 48])
        self.n1g = dr("n1g", [L, 128, KC])
        self.n2g = dr("n2g", [L, 128, KC])
        self.fing = dr("fing", [128, KC])
        self.w_in = dr("w_in", [L, D, INC])
        self.w_out = dr("w_out", [L, D, D])
        self.w_mlp1 = dr("w_mlp1", [L, D, DFF])
        self.w_mlp2 = dr("w_mlp2", [L, DFF, D])
        self.sink = dr("sink", [L, 8])
        self.mib = dr("mib", [L, 8])
        self.mfb = dr("mfb", [L, 8])
        self.mgn = dr("mgn", [L, 512])
        self.waug = dr("waug", [L, 33, 512])
        self.cf32 = dr("cf32", [128, 7 * 128])
        self.rope = dr("rope", [NTL, 128, 4, 64])
        self.outT = dr("outT", [NB, KC, 128, N], kind="ExternalOutput")
        I = "Internal"
        self.xres = dr("xres", [NB, KC, 128, T], kind=I)
        self.s_hf = dr("s_hf", [NB, NT, 128, 512], kind=I)
        self.s_oin = dr("s_oin", [NB, NT, 128, 520], kind=I)
        self.s_u = dr("s_u", [NB, NT, 64, 520], kind=I)
        self.s_dec = dr("s_dec", [NB, NT, 64, 8], kind=I)
        self.s_qT = dr("s_qT", [NB, NT, 64, 1024], BF16, kind=I)
        self.s_gate = dr("s_gate", [NB, NT, 128, 512], BF16, kind=I)
        self.s_att = dr("s_att", [NB, NT, 128, 512], BF16, kind=I)

        self.pb = [self.psb(f"pb{i}") for i in range(8)]
        cf = self.sb("cf", [128, 7 * 128])
        self.cf = cf
        self.dma(cf[:], self.cf32[:, :], W=[cf])
        self.ident_f = cf[:, 0:128]
        self.cU1, self.cL1 = cf[:, 384:512], cf[:, 512:640]
        self.cU16, self.cL16 = cf[:, 640:768], cf[:, 768:896]
        cb = self.sb("cb", [128, 4 * 128], BF16)
        self.cb = cb
        self.cp(self.dve, cb[:, 0:384], cf[:, 0:384], [cf], [cb])
        self.mset(self.dve, cb[:, 384:512], 1.0, [cb])
        self.ident_b = cb[:, 0:128]
        self.mU, self.mL = cb[:, 128:256], cb[:, 256:384]
        self.ones_b = cb[:, 384:512]
        cn = self.sb("cn", [128, 128])
        self.cn = cn
        self.mset(self.dve, cn[:, 0:64], -1.0, [cn])
        self.mset(self.dve, cn[:, 64:128], -1.0 / 16.0, [cn])

        cst = self.sb("cst", [128, 4])
        self.cst = cst
        self.mset(self.dve, cst[:, 0:1], EPS, [cst])
        self.mset(self.dve, cst[:, 1:2], 1.0, [cst])
        self.mset(self.dve, cst[:, 2:3], LN8, [cst])
        self.epsb = cst
        self.c_eps, self.c_one, self.c_ln8 = cst[:, 0:1], cst[:, 1:2], cst[:, 2:3]
        import os
        stop = os.environ.get("MK_STOP", "")
        self.modulation()
        for l in range(L):
            if stop == "mod":
                break
            self.phase1(l)
            if stop == "p1":
                break
            self.phase3a(l)
            if stop == "p3a":
                break
            self.phase3b(l)
            if stop == "p3b":
                break
        self.barrier()

    def modulation(self):
        nc = self.nc
        L, NG = self.DEPTH, self.NG
        self.mod = self.sb("mod", [128, L, 48, NG])
        self.A1 = self.sb("A1", [128, L, KC, NG])
        self.A2 = self.sb("A2", [128, L, KC, NG])
        with ExitStack() as st:
            cTs = self.sb("cTs", [128, KC, NG], stack=st)
            sc = self.sb("scT", [128, KC, NG], stack=st)
            tmp = self.sb("modtmp", [128, KC, NG], stack=st)
            stg = [self.sb(f"adastg{i}", [128, KC, 512], stack=st) for i in range(2)]
            bad = self.sb("bad", [128, L, 48], stack=st)
            ng = self.sb("ng", [128, 2, L, KC], stack=st)
            self.dma(cTs[:], self.cT[:, :, :], W=[cTs])
            self.dma(bad[:], self.b_adaT.rearrange("l p c -> p l c"), W=[bad])
            self.dma(ng[:, 0], self.n1g.rearrange("l p c -> p l c"), W=[ng])
            self.dma(ng[:, 1], self.n2g.rearrange("l p c -> p l c"), W=[ng])
            self.actf(tmp[:], cTs[:], AF.Exp, [cTs], [tmp], scale=-1.0)
            self.ts(self.dve, tmp[:], tmp[:], 1.0, ALU.add, [tmp], [tmp])
            self.recip(tmp[:], tmp[:], [tmp], [tmp])
            self.tt(self.dve, sc[:], cTs[:], tmp[:], ALU.mult, [cTs, tmp], [sc])
            psm = self.pb[0]
            i = 0
            for l in range(L):
                pv = psm[:, 0:48 * NG].rearrange("p (c g) -> p c g", g=NG)
                for cg in range(12):
                    s = stg[i % 2]
                    i += 1
                    self.dma(s[:], self.w_ada[l, :, cg * 512:(cg + 1) * 512].rearrange("(kc p) c -> p kc c", p=128),
                             W=[s])
                    for j in range(4):
                        for kc in range(KC):
                            self.mm(pv[:, cg * 4 + j, :], s[:, kc, j * 128:(j + 1) * 128], sc[:, kc, :],
                                    kc == 0, kc == KC - 1, [s, sc], [psm])
                self.tt(self.dve, self.mod[:, l], pv, bad[:, l].unsqueeze(2).broadcast_to([128, 48, NG]), ALU.add,
                        [psm, bad], [self.mod])
                for (A, which, gi) in ((self.A1, 1, 0), (self.A2, 4, 1)):
                    self.stt(A[:, l], self.mod[:, l, which * 8:(which + 1) * 8, :], 1.0,
                             ng[:, gi, l].unsqueeze(2).broadcast_to([128, KC, NG]), ALU.add, ALU.mult,
                             [self.mod, ng], [A])
            self.barrier()

    def modv(self, l, which, c, g):
        return self.mod[:, l, which * 8 + c, g:g + 1]

    def load_weight(self, dst, src_ap, nk, ncols_list, stg, R=()):
        for (s0, s1, d0) in ncols_list:
            for k0 in range(0, nk, KC):
                for c0 in range(s0, s1, 512):
                    c1 = min(c0 + 512, s1)
                    w = c1 - c0
                    s = stg[self.rr % 2]
                    self.dma(s[:, :, 0:w], src_ap[k0 * 128:(k0 + KC) * 128, c0:c1].rearrange("(kc p) c -> p kc c", p=128),
                             W=[s])
                    eng = self.rot([self.dve, self.pool, self.act])
                    dd = d0 + (c0 - s0)
                    self.cp(eng, dst[:, k0:k0 + KC, dd:dd + w], s[:, :, 0:w], [s], [dst])

    def norm_block(self, st, xblk, W_, A, Bv, out_bf, pbn, names):
        sq, rstd, tmpn = names
        self.tt(self.pool, sq[:, :, 0:W_], xblk[:, :, 0:W_], xblk[:, :, 0:W_], ALU.mult, [xblk], [sq])
        for c in range(KC):
            self.mm(pbn[:, 0:W_], self.ones_b, sq[:, c, 0:W_], c == 0, c == KC - 1, [sq, self.cb], [pbn])
        self.actf(rstd[:, 0:W_], pbn[:, 0:W_], AF.Ln, [pbn], [rstd], scale=1.0 / D, bias=self.c_eps)
        self.actf(rstd[:, 0:W_], rstd[:, 0:W_], AF.Exp, [rstd], [rstd], scale=-0.5)
        if out_bf is None:
            return
        for c in range(KC):
            self.tt(self.dve, tmpn[:, 0:W_], xblk[:, c, 0:W_], rstd[:, 0:W_], ALU.mult, [xblk, rstd], [tmpn])
            self.actf(out_bf[:, c, 0:W_], tmpn[:, 0:W_], AF.Identity, [tmpn, self.mod, self.A1, self.A2], [out_bf],
                      scale=A(c), bias=Bv(c))

    def phase1(self, l):
        nc = self.nc
        NB, NT, NTL, NTC, T, NG = self.NB, self.NT, self.NTL, self.NTC, self.T, self.NG
        last = (l == self.DEPTH - 1)
        pb = self.pb
        with ExitStack() as st:
            sb = lambda name, shape, dt=F32: self.sb(f"p1_{name}", shape, dt, stack=st)
            w_in = sb("w_in", [128, KC, INC], BF16)
            with ExitStack() as st2:
                stg = [self.sb(f"p1_stg{i}", [128, KC, 512], stack=st2) for i in range(2)]
                self.load_weight(w_in, self.w_in[l], KC, WIN_MAP, stg)
                self.barrier()
            self.ck("w")
            waug = sb("waug", [33, 512])
            self.dma(waug[:], self.waug[l], W=[waug])
            bi = sb("bi", [128, 8])
            bfz = sb("bfz", [128, 8])
            esink = sb("esink", [128, 8])
            self.dma(bi[:], self.mib[l].partition_broadcast(128), W=[bi])
            self.dma(bfz[:], self.mfb[l].partition_broadcast(128), W=[bfz])
            self.dma(esink[:], self.sink[l].partition_broadcast(128), W=[esink])
            self.actf(esink[:], esink[:], AF.Exp, [esink], [esink])
            xblk = [sb(f"xblk{i}", [128, KC, 256]) for i in range(2)]
            sq = sb("sq", [128, KC, 256], BF16)
            xhat = [sb(f"xhat{i}", [128, KC, 256], BF16) for i in range(2)]
            pr = [sb(f"pr{i}", [128, INC]) for i in range(2)]
            rstd = sb("rstd", [128, 256])
            tmpn = sb("tmpn", [128, 256])
            KT = sb("KT", [64, 2, T], BF16)
            ktok = [Tok(f"kt{t}") for t in range(NT)]
            VA = sb("VA", [128, NT, 2, 65], BF16)
            vtok = [Tok(f"vt{t}") for t in range(NT)]
            ropet = [sb(f"ropet{i}", [128, 4, 64]) for i in range(2)]
            t1 = sb("t1", [128, 512])
            t2 = sb("t2", [128, 512])
            q_r = sb("q_r", [128, 8, 64], BF16)
            k_r = sb("k_r", [128, 2, 64], BF16)
            t1k = sb("t1k", [128, 128])
            t2k = sb("t2k", [128, 128])
            qT = [sb(f"qT{i}", [64, 8, 128], BF16) for i in range(3)]
            logi = sb("logi", [128, 8])
            lm = sb("lm", [128, 8])
            glrT = sb("glrT", [33, 128])
            self.mset(self.dve, glrT[:], 1.0, [glrT])
            lg = sb("lg", [128, 512])
            ebm = sb("ebm", [128, 8])
            ksm = sb("ksm", [128, 8])
            ebg = sb("ebg", [128, 512])
            enbg = sb("enbg", [128, 512])
            dec = sb("dec", [64, 16])
            qp = [sb(f"qp{d}", [128, 8, 64], BF16) for d in range(2)]
            kp = [sb(f"kp{d}", [128, 8, 64], BF16) for d in range(2)]
            qpT = [sb(f"qpT{d}", [64, 8, 128], BF16) for d in range(2)]
            kpT = [sb(f"kpT{d}", [64, 8, 128], BF16) for d in range(2)]
            AT = [sb(f"AT{d}", [128, 8, 128], BF16) for d in range(2)]
            vaug = sb("vaug", [128, 8, 65], BF16)
            self.mset(self.dve, vaug[:], 1.0, [vaug])
            gates = sb("gates", [128, 512], BF16)
            gt1 = sb("gt1", [128, 256])
            gt2 = sb("gt2", [128, 256])
            P = [sb(f"P{i}", [128, 4, 128], BF16) for i in range(5)]
            hf = sb("hf", [128, 8, 64])
            oin = sb("oin", [128, 8, 65])
            usb = sb("usb", [64, 8, 65])
            Sst = sb("S", [64, 8, 65])
            Sbf = sb("Sbf", [64, 8, 65], BF16)
            stmp = sb("stmp", [64, 8, 65])
            dn = sb("dn", [128, 8, 1])
            ao = sb("ao", [128, 8, 64], BF16)
            aoT = sb("aoT", [128, 4, 128], BF16)
            self.mset(self.pool, VA[:], 1.0, [VA] + vtok)

            pbT = pb[2]
            pbTv = pbT[:].bitcast(BF16)

            dna = sb("dna", [128, 8, 1])

            def attention(b, tq, qTt, keys):
                pS, pv = pb[6], pb[7]
                for g in range(2):
                    for ci, (kt, msk) in enumerate(keys):
                        self.mm(pS[:, 0:512], KT[:, g, kt * 128:(kt + 1) * 128], qTt[:, 4 * g:4 * g + 4, :],
                                True, True, [ktok[kt], qTt], [pS])
                        Pv = P[ci][:].rearrange("p h q -> p (h q)")
                        self.actf(Pv, pS[:, 0:512], AF.Exp, [pS], [P[ci]])
                        if msk is not None:
                            self.tt(self.pool, P[ci][:], P[ci][:], msk.unsqueeze(1).broadcast_to([128, 4, 128]),
                                    ALU.mult, [P[ci], self.cb], [P[ci]])
                        yield
                    pvv = pv[:, 0:260].rearrange("p (h e) -> p h e", e=65)
                    for hg in range(4):
                        for ci, (kt, msk) in enumerate(keys):
                            self.mm(pvv[:, hg, :], P[ci][:, hg, :], VA[:, kt, g, :], ci == 0, ci == len(keys) - 1,
                                    [P[ci], vtok[kt]], [pv])
                        yield
                    self.tt(self.dve, dna[:, 4 * g:4 * g + 4, :], pvv[:, :, 64:65], esink[:, 4 * g:4 * g + 4].unsqueeze(2),
                            ALU.add, [pv, esink], [dna])
                    self.recip(dna[:, 4 * g:4 * g + 4, :], dna[:, 4 * g:4 * g + 4, :], [dna], [dna])
                    self.tt(self.dve, ao[:, 4 * g:4 * g + 4, :], pvv[:, :, 0:64],
                            dna[:, 4 * g:4 * g + 4, :].broadcast_to([128, 4, 64]), ALU.mult, [pv, dna], [ao])
                    yield
                aov = ao[:].rearrange("p h d -> p (h d)")
                pTa = pbTv[:, 0:512].rearrange("p (k t) -> p k t", t=128)
                for k in range(4):
                    self.tr(pTa[:, k, :], aov[:, k * 128:(k + 1) * 128], self.ident_b, [ao, self.cb], [pbT])
                self.cp(self.act, aoT[:], pTa, [pbT], [aoT])
                yield
                self.dma(self.s_att[b, tq].rearrange("p (k t) -> p k t", t=128), aoT[:], R=[aoT],
                         W=[self.dt("att", b, tq)])
                yield

            groups = [GA, GB, GC, GD, GE, GF]

            def load_x(b, blk):
                xb = xblk[blk % 2]
                src = self.xT0 if l == 0 else self.xres
                self.dma(xb[:], src[b, :, :, blk * 256:(blk + 1) * 256].rearrange("c p t -> p c t"),
                         R=[self.dt("x", b, blk)], W=[xb])

            def do_norm(b, blk):
                isctx = (blk * 2 < NTC)
                g = NB if isctx else b
                self.norm_block(st, xblk[blk % 2], 256, lambda c: self.A1[:, l, c, g:g + 1],
                                lambda c: self.modv(l, 0, c, g), xhat[blk % 2], pb[4], (sq, rstd, tmpn))

            def proj_thread(b, t):
                blk, j = t // 2, t % 2
                if j == 0:
                    if blk + 1 < NT // 2:
                        load_x(b, blk + 1)
                    do_norm(b, blk)
                    yield
                xh = xhat[blk % 2][:, :, j * 128:(j + 1) * 128]
                prj = pr[t % 2]
                for gi, G in enumerate(groups):
                    pbk = pb[self.npj % 2]
                    self.npj += 1
                    for kc in range(KC):
                        self.mm(pbk[:, 0:G[1] - G[0]], xh[:, kc, :], w_in[:, kc, G[0]:G[1]], kc == 0,
                                kc == KC - 1, [xhat[blk % 2], w_in], [pbk])
                    yield
                    ev = self.act if gi % 3 != 2 else self.dve
                    self.cp(ev, prj[:, G[0]:G[1]], pbk[:, 0:G[1] - G[0]], [pbk], [prj])
                    yield

            pbA = pb[5][:].bitcast(BF16)

            def post(b, t, part):
                prj = pr[t % 2]
                isctx = (t < NTC)
                tl = t - NTC
                qTt = qT[t % 3]
                if not isctx:
                    rp = ropet[t % 2]
                    if part == "a":
                        self.dma(rp[:], self.rope[tl], W=[rp])

                def rope_apply(c0, H, tab, o1, o2, outv):
                    psv = prj[:, c0:c0 + H * 64]
                    p3 = psv.rearrange("p (h d) -> p h d", d=64)
                    p5 = psv.rearrange("p (h a f j) -> p h a f j", a=2, f=2, j=16)
                    cs = rp[:, tab[0], :].unsqueeze(1).broadcast_to([128, H, 64])
                    sn = rp[:, tab[1], :].rearrange("p (a f j) -> p a f j", a=2, f=2)
                    o1v = o1[:, 0:H * 64].rearrange("p (h d) -> p h d", d=64)
                    o25 = o2[:, 0:H * 64].rearrange("p (h a f j) -> p h a f j", a=2, f=2, j=16)
                    self.tt(self.dve, o1v, p3, cs, ALU.mult, [prj, rp], [o1])
                    for f in range(2):
                        self.tt(self.pool, o25[:, :, :, f, :], p5[:, :, :, 1 - f, :],
                                sn[:, :, f, :].unsqueeze(1).broadcast_to([128, H, 2, 16]), ALU.mult,
                                [prj, rp], [o2])
                    self.tt(self.dve, outv[:].rearrange("p h d -> p (h d)"), o1[:, 0:H * 64], o2[:, 0:H * 64],
                            ALU.add, [o1, o2], [outv])

                if part == "a":
                    if isctx:
                        self.actf(q_r[:].rearrange("p h d -> p (h d)"), prj[:, 0:512], AF.Copy, [prj], [q_r],
                                  scale=0.125)
                        self.cp(self.pool, k_r[:].rearrange("p h d -> p (h d)"), prj[:, 512:640], [prj], [k_r])
                    else:
                        rope_apply(0, 8, (0, 1), t1, t2, q_r)
                        yield
                        rope_apply(512, 2, (2, 3), t1k, t2k, k_r)
                    yield
                    pT8 = pbA[0:64, :].rearrange("p (h t) -> p h t", t=128)
                    for h in range(8):
                        self.tr(pT8[:, h, :], q_r[:, h, :], self.ident_b, [q_r, self.cb], [pb[5]])
                    self.cp(self.act, qTt[:], pT8, [pb[5]], [qTt])
                    yield
                    self.cp(self.pool, VA[:, t, :, 0:64], prj[:, 640:768].rearrange("p (h d) -> p h d", d=64),
                            [prj], [vtok[t]])
                    pT2 = pbA[0:64, 0:256].rearrange("p (h t) -> p h t", t=128)
                    for h in range(2):
                        self.tr(pT2[:, h, :], k_r[:, h, :], self.ident_b, [k_r, self.cb], [pb[5]])
                    self.cp(self.act, KT[:, :, t * 128:(t + 1) * 128], pT2, [pb[5]], [ktok[t]])
                    yield
                    return
                self.actf(gt1[:], prj[:, 1584:1840], AF.Exp, [prj], [gt1], scale=-1.0)
                self.actf(gt2[:], prj[:, 2608:2864], AF.Exp, [prj], [gt2], scale=-1.0)
                self.actf(gt1[:], gt1[:], AF.Ln, [gt1], [gt1], bias=self.c_one)
                self.actf(gt2[:], gt2[:], AF.Ln, [gt2], [gt2], bias=self.c_one)
                self.tt(self.dve, logi[:], prj[:, 768:776], bi[:], ALU.add, [prj, bi], [logi])
                self.tt(self.dve, lm[:], prj[:, 776:784], bfz[:], ALU.add, [prj, bfz], [lm])
                yield
                self.actf(gates[:, 0:256], gt1[:], AF.Exp, [gt1], [gates], scale=-1.0)
                self.actf(gt2[:], gt2[:], AF.Exp, [gt2], [gt2], scale=-1.0)
                yield
                self.tt(self.dve, gates[:, 256:512], prj[:, 2608:2864], gt2[:], ALU.mult, [prj, gt2], [gates])
                self.dma(self.s_gate[b, t], gates[:], R=[gates], W=[self.dt("gate", b, t)])
                self.actf(lm[:], lm[:], AF.Exp, [lm], [lm], scale=-1.0)
                self.actf(lm[:], lm[:], AF.Ln, [lm], [lm], bias=self.c_one)
                self.tr(pb[4][0:32, 128:256], prj[:, 784:816], self.ident_f, [prj, self.cf], [pb[4]])
                self.cp(self.dve, glrT[0:32, :], pb[4][0:32, 128:256], [pb[4]], [glrT])
                yield
                self.mm(pb[3][:, 0:512], glrT[:], waug[:], True, True, [glrT, waug], [pb[3]])
                self.actf(lg[:], pb[3][:, 0:512], AF.Exp, [pb[3]], [lg], scale=-1.0)
                self.actf(lg[:], lg[:], AF.Ln, [lg], [lg], bias=self.c_one)
                yield
                self.mm(pb[4][:, 0:4], self.cU1, lm[:, 0:4], True, True, [lm, self.cf], [pb[4]])
                self.mm(pb[4][:, 4:8], self.cL1, lm[:, 4:8], True, True, [lm, self.cf], [pb[4]])
                self.mm(pb[3][:, 0:256], self.cU16, lg[:, 0:256], True, True, [lg, self.cf], [pb[3]])
                self.mm(pb[3][:, 256:512], self.cL16, lg[:, 256:512], True, True, [lg, self.cf], [pb[3]])
                pdec = pb[4][0:64, 16:32].rearrange("p (d e) -> p d e", e=8)
                for d_ in range(2):
                    self.mm(pdec[:, d_, 0:4], self.cn[:, 0:64], lm[:, 4 * d_:4 * d_ + 4], True, True,
                            [lm, self.cn], [pb[4]])
                    for h in range(4):
                        self.mm(pdec[:, d_, 4 + h:5 + h], lg[:, d_ * 256 + h * 64:d_ * 256 + (h + 1) * 64],
                                self.cn[:, 64:65], True, True, [lg, self.cn], [pb[4]])
                yield
                self.actf(ebg[:], pb[3][:, 0:512], AF.Exp, [pb[3]], [ebg])
                self.actf(enbg[:], pb[3][:, 0:512], AF.Exp, [pb[3]], [enbg], scale=-1.0)
                self.actf(dec[:], pb[4][0:64, 16:32], AF.Exp, [pb[4]], [dec])
                self.actf(ebm[:], pb[4][:, 0:8], AF.Exp, [pb[4]], [ebm])
                self.actf(ksm[:], pb[4][:, 0:8], AF.Copy, [pb[4]], [ksm], scale=-1.0)
                self.tt(self.dve, ksm[:], ksm[:], logi[:], ALU.add, [logi, ksm], [ksm])
                self.actf(ksm[:], ksm[:], AF.Exp, [ksm], [ksm], bias=self.c_ln8)
                yield
                self.cp(self.act, vaug[:, 0:4, 0:64], prj[:, 1328:1584].rearrange("p (h d) -> p h d", d=64),
                        [prj], [vaug])
                self.cp(self.act, vaug[:, 4:8, 0:64], prj[:, 2352:2608].rearrange("p (h d) -> p h d", d=64),
                        [prj], [vaug])
                for d_ in range(2):
                    self.tt(self.dve, qp[d_][:, 0:4, :], prj[:, 816:1072].rearrange("p (h d) -> p h d", d=64),
                            ebm[:, 4 * d_:4 * d_ + 4].unsqueeze(2).broadcast_to([128, 4, 64]), ALU.mult,
                            [prj, ebm], [qp[d_]])
                    self.tt(self.pool, kp[d_][:, 0:4, :], prj[:, 1072:1328].rearrange("p (h d) -> p h d", d=64),
                            ksm[:, 4 * d_:4 * d_ + 4].unsqueeze(2).broadcast_to([128, 4, 64]), ALU.mult,
                            [prj, ksm], [kp[d_]])
                    self.stt(qp[d_][:, 4:8, :].rearrange("p h d -> p (h d)"), prj[:, 1840:2096], 0.125,
                             ebg[:, d_ * 256:(d_ + 1) * 256], ALU.mult, ALU.mult, [prj, ebg], [qp[d_]])
                    self.tt(self.pool, kp[d_][:, 4:8, :].rearrange("p h d -> p (h d)"), prj[:, 2096:2352],
                            enbg[:, d_ * 256:(d_ + 1) * 256], ALU.mult, [prj, enbg], [kp[d_]])
                    yield
                for d_ in range(2):
                    pq = pbTv[0:64, :].rearrange("p (h t) -> p h t", t=128)
                    for h in range(8):
                        self.tr(pq[:, h, :], qp[d_][:, h, :], self.ident_b, [qp[d_], self.cb], [pbT])
                    self.cp(self.act, qpT[d_][:], pq, [pbT], [qpT[d_]])
                    yield
                    for h in range(8):
                        self.tr(pq[:, h, :], kp[d_][:, h, :], self.ident_b, [kp[d_], self.cb], [pbT])
                    self.cp(self.dve, kpT[d_][:], pq, [pbT], [kpT[d_]])
                    yield
                    msk = self.mU if d_ == 0 else self.mL
                    for half in range(2):
                        pA = pb[5]
                        pAv = pA[:, 0:512].rearrange("p (h r) -> p h r", r=128)
                        for hh in range(4):
                            h = half * 4 + hh
                            self.mm(pAv[:, hh, :], kpT[d_][:, h, :], qpT[d_][:, h, :], True, True,
                                    [kpT[d_], qpT[d_]], [pA])
                        self.tt(self.dve, AT[d_][:, half * 4:half * 4 + 4, :], pAv,
                                msk.unsqueeze(1).broadcast_to([128, 4, 128]), ALU.mult, [pA, self.cb], [AT[d_]])
                        yield
                    for half in range(2):
                        pO = pb[3]
                        pOv = pO[:, 0:260].rearrange("p (h e) -> p h e", e=65)
                        pU = pb[4]
                        pUv = pU[0:64, 0:260].rearrange("p (h e) -> p h e", e=65)
                        for hh in range(4):
                            h = half * 4 + hh
                            self.mm(pOv[:, hh, :], AT[d_][:, h, :], vaug[:, h, :], True, d_ == 1,
                                    [AT[d_], vaug], [pO])
                            if d_ == 0:
                                self.mm(pOv[:, hh, :], qpT[0][:, h, :], Sbf[:, h, :], False, True,
                                        [qpT[0], Sbf], [pO])
                        for hh in range(4):
                            h = half * 4 + hh
                            self.mm(pUv[:, hh, :], kp[d_][:, h, :], vaug[:, h, :], True, True,
                                    [kp[d_], vaug], [pU])
                        yield
                        hs = slice(half * 4, half * 4 + 4)
                        if d_ == 0:
                            if half == 0:
                                self.ts(self.dve, dn[:, 0:4, :], pOv[:, :, 64:65], -1.0, ALU.mult, [pO], [dn],
                                        s2=1.0, op1=ALU.max)
                                self.tt(self.dve, dn[:, 0:4, :], dn[:, 0:4, :], pOv[:, :, 64:65], ALU.max,
                                        [pO, dn], [dn])
                                self.recip(dn[:, 0:4, :], dn[:, 0:4, :], [dn], [dn])
                                self.tt(self.dve, hf[:, 0:4, :], pOv[:, :, 0:64],
                                        dn[:, 0:4, :].broadcast_to([128, 4, 64]), ALU.mult, [pO, dn], [hf])
                            else:
                                self.cp(self.act, hf[:, 4:8, :], pOv[:, :, 0:64], [pO], [hf])
                            self.tt(self.dve, stmp[:, hs, :], pUv, Sst[:, hs, :], ALU.add, [pU, Sst], [stmp])
                            self.tt(self.pool, Sst[:, hs, :], stmp[:, hs, :],
                                    dec[:, half * 4:half * 4 + 4].unsqueeze(2).broadcast_to([64, 4, 65]),
                                    ALU.mult, [stmp, dec], [Sst])
                            self.cp(self.act, Sbf[:, hs, :], Sst[:, hs, :], [Sst], [Sbf])
                        else:
                            self.cp(self.act, oin[:, hs, :], pOv, [pO], [oin])
                            self.cp(self.dve, usb[:, hs, :], pUv, [pU], [usb])
                        yield
                    if d_ == 0:
                        self.dma(self.s_hf[b, t], hf[:].rearrange("p h d -> p (h d)"), R=[hf],
                                 W=[self.dt("hf", b, t)])
                    else:
                        self.dma(self.s_oin[b, t], oin[:].rearrange("p h e -> p (h e)"), R=[oin],
                                 W=[self.dt("oin", b, t)])
                        self.dma(self.s_u[b, t], usb[:].rearrange("p h e -> p (h e)"), R=[usb],
                                 W=[self.dt("u", b, t)])
                        self.dma(self.s_dec[b, t], dec[:, 8:16], R=[dec], W=[self.dt("dec", b, t)])
                        self.dma(self.s_qT[b, t], qpT[1][:].rearrange("p h t -> p (h t)"), R=[qpT[1]],
                                 W=[self.dt("qT", b, t)])
                    yield

            ctxkeys = [(c, None) for c in range(NTC)]

            def lat_keys(tq):
                ks = []
                if tq - 1 >= NTC:
                    ks.append((tq - 1, self.mL))
                ks.append((tq, None))
                if tq + 1 < NT:
                    ks.append((tq + 1, self.mU))
                return ks + ctxkeys

            def att_jobs(b, p):
                if p == NTC and not last:
                    for tq in range(NTC):
                        yield from attention(b, tq, qT[tq % 3], ctxkeys)
                tq = p - 2
                if tq >= NTC and tq < NT:
                    yield from attention(b, tq, qT[tq % 3], lat_keys(tq))

            self.npj = 0
            for b in range(NB):
                self.mset(self.dve, Sst[:], 0.0, [Sst])
                self.mset(self.dve, Sbf[:], 0.0, [Sbf])
                load_x(b, 0)
                for t in range(NT + 3):
                    p = t - 1
                    threads = []
                    if t < NT:
                        threads.append(proj_thread(b, t))
                    if 0 <= p < NT:
                        threads.append(post(b, p, "a"))
                        threads.append(post(b, p, "b"))
                    if p >= NTC:
                        threads.append(att_jobs(b, p))
                    while threads:
                        for th in list(threads):
                            try:
                                next(th)
                            except StopIteration:
                                threads.remove(th)
            self.barrier()

    def phase3a(self, l):
        nc = self.nc
        NB, NT, NTL, NTC, T, NG = self.NB, self.NT, self.NTL, self.NTC, self.T, self.NG
        last = (l == self.DEPTH - 1)
        pb = self.pb
        with ExitStack() as st:
            sb = lambda name, shape, dt=F32: self.sb(f"p3_{name}", shape, dt, stack=st)
            w_out = sb("w_out", [128, KC, D], BF16)
            with ExitStack() as st2:
                stg = [self.sb(f"p3_stg{i}", [128, KC, 512], stack=st2) for i in range(2)]
                self.load_weight(w_out, self.w_out[l], KC, [(0, D, 0)], stg)
                self.barrier()
            mgn = sb("mgn", [128, 512])
            self.dma(mgn[:], self.mgn[l].partition_broadcast(128), W=[mgn])

            def batch_thread(b):
                bb = b % 2
                n = lambda x: f"{x}_b{b}"
                xblk = [sb(n(f"xblk{i}"), [128, KC, 256]) for i in range(2)]
                oT = [sb(n(f"oT{i}"), [128, KC, 256], BF16) for i in range(2)]
                U = [sb(n(f"U{i}"), [64, 8, 65]) for i in range(2)]
                dec = [sb(n(f"dec{i}"), [64, 8]) for i in range(2)]
                qTl = [sb(n(f"qTl{i}"), [64, 8, 128], BF16) for i in range(2)]
                oin = [sb(n(f"oin{i}"), [128, 8, 65]) for i in range(2)]
                hf = [sb(n(f"hf{i}"), [128, 8, 64]) for i in range(2)]
                gat = [sb(n(f"gat{i}"), [128, 512], BF16) for i in range(2)]
                orev = sb(n("orev"), [128, 8, 65])
                hs = sb(n("hs"), [128, 8, 64])
                hsq = sb(n("hsq"), [128, 8, 64])
                ss = sb(n("ss"), [128, 8])
                dn = sb(n("dn"), [128, 4, 1])
                o1 = sb(n("o1"), [128, 8, 64])
                omg = sb(n("omg"), [128, 512], BF16)
                Sst = sb(n("S"), [64, 8, 65])
                Sbf = sb(n("Sbf"), [64, 8, 65], BF16)
                stmp = sb(n("stmp"), [64, 8, 65])
                pbT = pb[2 + bb]
                pbTv = pbT[:].bitcast(BF16)
                pO = pb[4 + bb]
                pks = [pb[0], pb[6]] if bb == 0 else [pb[1], pb[7]]
                pr = 0
                tiles = []
                order = list(range(NTC // 2 - 1, -1, -1)) + list(range(NT // 2 - 1, NTC // 2 - 1, -1))
                for blk in order:
                    for j in (1, 0):
                        tiles.append((blk, j, len(tiles) // 2))

                def loads(k):
                    blk, j, bi_ = tiles[k]
                    isctx = (blk * 2 < NTC)
                    full = not (isctx and last)
                    t = blk * 2 + j
                    i2 = k % 2
                    if full and j == 1:
                        src = self.xT0 if l == 0 else self.xres
                        self.dma(xblk[bi_ % 2][:], src[b, :, :, blk * 256:(blk + 1) * 256].rearrange("c p t -> p c t"),
                                 R=[self.dt("x", b, blk)], W=[xblk[bi_ % 2]])
                    self.dma(U[i2][:].rearrange("p h e -> p (h e)"), self.s_u[b, t], R=[self.dt("u", b, t)], W=[U[i2]])
                    self.dma(dec[i2][:], self.s_dec[b, t], R=[self.dt("dec", b, t)], W=[dec[i2]])
                    if full:
                        self.dma(qTl[i2][:].rearrange("p h t -> p (h t)"), self.s_qT[b, t], R=[self.dt("qT", b, t)],
                                 W=[qTl[i2]])
                        self.dma(oin[i2][:].rearrange("p h e -> p (h e)"), self.s_oin[b, t], R=[self.dt("oin", b, t)],
                                 W=[oin[i2]])
                        self.dma(hf[i2][:].rearrange("p h d -> p (h d)"), self.s_hf[b, t], R=[self.dt("hf", b, t)],
                                 W=[hf[i2]])
                        self.dma(gat[i2][:], self.s_gate[b, t], R=[self.dt("gate", b, t)], W=[gat[i2]])
                        self.dma(oT[bi_ % 2][:, 0:4, j * 128:(j + 1) * 128],
                                 self.s_att[b, t].rearrange("p (k t) -> p k t", t=128),
                                 R=[self.dt("att", b, t)], W=[oT[bi_ % 2]])

                self.mset(self.dve, Sst[:], 0.0, [Sst])
                self.mset(self.dve, Sbf[:], 0.0, [Sbf])
                loads(0)
                yield
                for k, (blk, j, bi_) in enumerate(tiles):
                    if k + 1 < len(tiles):
                        loads(k + 1)
                    isctx = (blk * 2 < NTC)
                    g = NB if isctx else b
                    full = not (isctx and last)
                    xb = xblk[bi_ % 2]
                    oTb = oT[bi_ % 2]
                    t = blk * 2 + j
                    i2 = k % 2
                    Ut, dect, qTt, oint, hft, gatt = U[i2], dec[i2], qTl[i2], oin[i2], hf[i2], gat[i2]
                    if full:
                        pOv = pO[:, 0:260].rearrange("p (h e) -> p h e", e=65)
                        for half in range(2):
                            hsl = slice(half * 4, half * 4 + 4)
                            for hh in range(4):
                                h = half * 4 + hh
                                self.mm(pOv[:, hh, :], qTt[:, h, :], Sbf[:, h, :], True, True, [qTt, Sbf], [pO])
                            yield
                            self.tt(self.dve, orev[:, hsl, :], pOv, oint[:, hsl, :], ALU.add, [pO, oint], [orev])
                            yield
                    self.tt(self.pool, stmp[:], Sst[:], Ut[:], ALU.add, [Sst, Ut], [stmp])
                    yield
                    self.tt(self.pool, Sst[:], stmp[:], dect[:].unsqueeze(2).broadcast_to([64, 8, 65]), ALU.mult,
                            [stmp, dect], [Sst])
                    yield
                    self.cp(self.act, Sbf[:], Sst[:], [Sst], [Sbf])
                    yield
                    if not full:
                        continue
                    self.ts(self.dve, dn[:], orev[:, 0:4, 64:65], -1.0, ALU.mult, [orev], [dn], s2=1.0, op1=ALU.max)
                    self.tt(self.dve, dn[:], dn[:], orev[:, 0:4, 64:65], ALU.max, [orev, dn], [dn])
                    yield
                    self.recip(dn[:], dn[:], [dn], [dn])
                    self.tt(self.pool, hs[:, 4:8, :], orev[:, 4:8, 0:64], hft[:, 4:8, :], ALU.add, [orev, hft], [hs])
                    yield
                    self.tt(self.dve, hs[:, 0:4, :], orev[:, 0:4, 0:64], dn[:].broadcast_to([128, 4, 64]), ALU.mult,
                            [orev, dn], [hs])
                    yield
                    self.tt(self.pool, hs[:, 0:4, :], hs[:, 0:4, :], hft[:, 0:4, :], ALU.add, [hs, hft], [hs])
                    yield
                    self.tt(self.pool, hsq[:], hs[:], hs[:], ALU.mult, [hs], [hsq])
                    yield
                    self.op(self.dve, lambda: nc.vector.tensor_reduce(out=ss[:], in_=hsq[:], axis=AX.X, op=ALU.add),
                            [hsq], [ss])
                    yield
                    self.actf(ss[:], ss[:], AF.Ln, [ss], [ss], scale=1.0 / HD, bias=self.c_eps)
                    self.actf(ss[:], ss[:], AF.Exp, [ss], [ss], scale=-0.5)
                    yield
                    self.tt(self.dve, o1[:], hs[:], ss[:].unsqueeze(2).broadcast_to([128, 8, 64]), ALU.mult,
                            [hs, ss], [o1])
                    yield
                    o1v = o1[:].rearrange("p h d -> p (h d)")
                    self.tt(self.pool, o1v, o1v, mgn[:], ALU.mult, [o1, mgn], [o1])
                    yield
                    self.tt(self.dve, omg[:], o1v, gatt[:], ALU.mult, [o1, gatt], [omg])
                    yield
                    pTa = pbTv[:, 0:512].rearrange("p (k t) -> p k t", t=128)
                    for kk in range(4):
                        self.tr(pTa[:, kk, :], omg[:, kk * 128:(kk + 1) * 128], self.ident_b, [omg, self.cb], [pbT])
                    yield
                    self.cp(self.act, oTb[:, 4:8, j * 128:(j + 1) * 128], pTa, [pbT], [oTb])
                    yield
                    if j != 0:
                        continue
                    for oc in range(KC):
                        pk = pks[pr % 2]
                        pr += 1
                        for kc in range(KC):
                            self.mm(pk[:, 0:256], w_out[:, kc, oc * 128:(oc + 1) * 128], oTb[:, kc, :], kc == 0,
                                    kc == KC - 1, [w_out, oTb], [pk])
                        yield
                        self.stt(xb[:, oc, :], pk[:, 0:256], self.modv(l, 2, oc, g), xb[:, oc, :], ALU.mult, ALU.add,
                                 [pk, xb, self.mod], [xb])
                        yield
                    self.dma(self.xres[b, :, :, blk * 256:(blk + 1) * 256].rearrange("c p t -> p c t"), xb[:],
                             R=[xb], W=[self.dt("x", b, blk)])
                    yield

            for b0 in range(0, NB, 2):
                threads = [batch_thread(b) for b in range(b0, min(b0 + 2, NB))]
                while threads:
                    for th in list(threads):
                        try:
                            next(th)
                        except StopIteration:
                            threads.remove(th)
            self.barrier()

    def phase3b(self, l):
        nc = self.nc
        NB, NT, NTL, NTC, T, NG = self.NB, self.NT, self.NTL, self.NTC, self.T, self.NG
        last = (l == self.DEPTH - 1)
        pb = self.pb
        with ExitStack() as st:
            sb = lambda name, shape, dt=F32: self.sb(f"p4_{name}", shape, dt, stack=st)
            w1 = sb("w1", [128, KC, DFF], BF16)
            w2 = sb("w2", [128, HC, D], BF16)
            with ExitStack() as st2:
                stg = [self.sb(f"p4_stg{i}", [128, KC, 512], stack=st2) for i in range(2)]
                self.load_weight(w1, self.w_mlp1[l], KC, [(0, DFF, 0)], stg)
                self.load_weight(w2, self.w_mlp2[l], HC, [(0, D, 0)], stg)
                self.barrier()
            fg = sb("fg", [128, KC])
            self.dma(fg[:], self.fing[:, :], W=[fg])
            xblk = [sb(f"xblk{i}", [128, KC, 256]) for i in range(2)]
            hT = [sb(f"hT{i}", [128, KC, 256], BF16) for i in range(2)]
            sq = sb("sq", [128, KC, 256], BF16)
            hid = sb("hid", [128, HC, 256], BF16)
            rstd = sb("rstd", [128, 256])
            tmpn = sb("tmpn", [128, 256])
            rl = [sb(f"rl{i}", [128, 256]) for i in range(4)]
            yout = [sb(f"yout{i}", [128, 256]) for i in range(4)]
            sq2 = sb("sq2", [128, KC, 256], BF16)
            rstd2 = sb("rstd2", [128, 256])
            pr = 0
            pbs = [pb[0], pb[1], pb[4], pb[5]]
            blocks = []
            for b in range(NB):
                for blk in range(NT // 2):
                    if (blk * 2 < NTC) and last:
                        continue
                    blocks.append((b, blk))

            def load_x(i):
                b, blk = blocks[i]
                self.dma(xblk[i % 2][:], self.xres[b, :, :, blk * 256:(blk + 1) * 256].rearrange("c p t -> p c t"),
                         R=[self.dt("x", b, blk)], W=[xblk[i % 2]])

            def do_norm(i):
                b, blk = blocks[i]
                g = NB if (blk * 2 < NTC) else b
                self.norm_block(st, xblk[i % 2], 256, lambda c: self.A2[:, l, c, g:g + 1],
                                lambda c: self.modv(l, 3, c, g), hT[i % 2], pb[3], (sq, rstd, tmpn))

            load_x(0)
            do_norm(0)
            for i, (b, blk) in enumerate(blocks):
                g = NB if (blk * 2 < NTC) else b
                xb = xblk[i % 2]
                hTb = hT[i % 2]
                if i + 1 < len(blocks):
                    load_x(i + 1)
                for hc in range(HC):
                    pk = pbs[pr % 4]
                    pr += 1
                    for kc in range(KC):
                        self.mm(pk[:, 0:256], w1[:, kc, hc * 128:(hc + 1) * 128], hTb[:, kc, :], kc == 0,
                                kc == KC - 1, [w1, hTb], [pk])
                    r_ = rl[hc % 4]
                    self.actf(r_[:], pk[:, 0:256], AF.Relu, [pk], [r_])
                    self.tt(self.pool, hid[:, hc, :], r_[:], r_[:], ALU.mult, [r_], [hid])
                if i + 1 < len(blocks):
                    do_norm(i + 1)
                for oc in range(KC):
                    pk = pbs[pr % 4]
                    pr += 1
                    for hc in range(HC):
                        self.mm(pk[:, 0:256], w2[:, hc, oc * 128:(oc + 1) * 128], hid[:, hc, :], hc == 0,
                                hc == HC - 1, [w2, hid], [pk])
                    self.stt(xb[:, oc, :], pk[:, 0:256], self.modv(l, 5, oc, g), xb[:, oc, :], ALU.mult, ALU.add,
                             [pk, xb, self.mod], [xb])
                if not last:
                    self.dma(self.xres[b, :, :, blk * 256:(blk + 1) * 256].rearrange("c p t -> p c t"), xb[:],
                             R=[xb], W=[self.dt("x", b, blk)])
                else:
                    self.norm_block(st, xb, 256, None, None, None, pb[6], (sq2, rstd2, tmpn))
                    lb = blk - NTC // 2
                    for c in range(KC):
                        y = yout[c % 4]
                        self.stt(y[:], xb[:, c, :], fg[:, c:c + 1], rstd2[:], ALU.mult, ALU.mult, [xb, fg, rstd2], [y])
                        self.dma(self.outT[b, c, :, lb * 256:(lb + 1) * 256], y[:], R=[y],
                                 W=[self.dt("out", b, blk, c)])
            self.barrier()


def host_constants(NTL):
    s = np.arange(128)
    U = (s[:, None] <= s[None, :]).astype(np.float32)
    Lm = (s[:, None] >= s[None, :]).astype(np.float32)
    cf = np.concatenate([np.eye(128, dtype=np.float32), U, Lm, -U, -Lm, -U / 16.0, -Lm / 16.0], axis=1)
    t = np.arange(NTL * 128)
    inv = (10000.0 ** (-np.arange(0, 32, 2, dtype=np.float32) / 32.0)).astype(np.float32)
    ang_r = (t // 64).astype(np.float32)[:, None] * inv[None, :]
    ang_c = (t % 64).astype(np.float32)[:, None] * inv[None, :]
    cr, sr, cc, sc_ = np.cos(ang_r), np.sin(ang_r), np.cos(ang_c), np.sin(ang_c)
    CS = np.concatenate([cr, cr, cc, cc], axis=1).astype(np.float32)
    SN = np.concatenate([-sr, sr, -sc_, sc_], axis=1).astype(np.float32)
    rope = np.stack([CS * 0.125, SN * 0.125, CS, SN], axis=1).reshape(NTL, 128, 4, 64)
    return cf, np.ascontiguousarray(rope.astype(np.float32))


_CACHE = {}


def run(inputs, NB, n_cores, DEPTH):
    x = np.asarray(inputs["x"], dtype=np.float32)
    ctx = np.asarray(inputs["ctx"], dtype=np.float32)
    B, N, _ = x.shape
    NCX = ctx.shape[1]
    NTL, NTC = N // 128, NCX // 128
    T = N + NCX
    key = (NB, NTL, NTC, DEPTH)
    if key not in _CACHE:
        mk = MK(NB, NTL, NTC, DEPTH)
        mk.build()
        _CACHE[key] = mk
    mk = _CACHE[key]
    f = lambda k: np.asarray(inputs[k], dtype=np.float32)
    L = DEPTH
    cf, rope = host_constants(NTL)
    xall = np.concatenate([ctx, x], axis=1)
    xT = np.ascontiguousarray(xall.transpose(0, 2, 1)).reshape(B, KC, 128, T)
    fm = lambda v: np.ascontiguousarray(v.reshape(-1, KC, 128).transpose(0, 2, 1))
    waug = np.zeros((L, 33, 512), np.float32)
    g_wa2, g_ba = f("g_wa2"), f("g_ba")
    waug[:, 0:16, 0:256] = g_wa2[:, 0]
    waug[:, 16:32, 256:512] = g_wa2[:, 1]
    waug[:, 32, 0:256] = g_ba[:, 0]
    waug[:, 32, 256:512] = g_ba[:, 1]
    shared = {
        "w_ada": f("w_ada"), "b_adaT": np.ascontiguousarray(f("b_ada").reshape(L, 48, 128).transpose(0, 2, 1)),
        "n1g": fm(f("norm1_g")), "n2g": fm(f("norm2_g")), "fing": fm(f("final_g")[None])[0],
        "w_in": f("w_in"), "w_out": f("w_out"), "w_mlp1": f("w_mlp1"), "w_mlp2": f("w_mlp2"),
        "sink": f("attn_sink"), "mib": f("m_i_bias").reshape(L, 8), "mfb": f("m_f_bias").reshape(L, 8),
        "mgn": np.concatenate([f("m_norm_g"), f("g_norm_g")], axis=1), "waug": waug,
        "cf32": cf, "rope": rope,
    }
    c, c_ctx = f("c"), f("c_ctx")
    in_maps = []
    for i in range(n_cores):
        bs = slice(i * NB, (i + 1) * NB)
        cc = np.concatenate([c[bs], c_ctx[None]], axis=0)
        cT = np.ascontiguousarray(cc.reshape(NB + 1, KC, 128).transpose(2, 1, 0))
        m = dict(shared)
        m["xT0"] = xT[bs]
        m["cT"] = cT
        in_maps.append(m)
    res = run_bass_kernel_spmd(mk.nc, in_maps, core_ids=list(range(n_cores)))
    outT = np.concatenate([r["outT"] for r in res.results], axis=0)
    return np.ascontiguousarray(outT.reshape(B, D, N).transpose(0, 2, 1))


def kernel(**inputs):
    return run(inputs, NB=2, n_cores=8, DEPTH=4)
```
